# Optimizing a Trainium2 kernel written in Bass

```python
import jax, jax.numpy as jnp
from jax import lax
import numpy as np

D_MODEL = 2048
BATCH = 4
SEQ = 4096
DEPTH = 2

N_MIXERS = 2
EPS = 1e-6
CONV_WIDTH = 4
SSD_EXPAND = 2
SSD_D_INNER = SSD_EXPAND * D_MODEL
SSD_HEAD_DIM = 64
SSD_HEADS = SSD_D_INNER // SSD_HEAD_DIM
SSD_GROUPS = 8
SSD_HEADS_PER_GROUP = SSD_HEADS // SSD_GROUPS
SSD_D_STATE = 128
SSD_CHUNK = 128
SSD_BC_WIDTH = SSD_GROUPS * SSD_D_STATE
SSD_CONV_DIM = SSD_D_INNER + 2 * SSD_BC_WIDTH
SSD_PROJ = SSD_D_INNER + SSD_CONV_DIM + SSD_HEADS
ML_D_INNER = 2 * D_MODEL
ML_HEADS = 8
ML_V_DIM = ML_D_INNER // ML_HEADS
ML_QK_DIM = ML_V_DIM // 2
ML_QK_WIDTH = 2 * ML_HEADS * ML_QK_DIM
ML_CHUNK = 64
ML_PROJ = ML_QK_WIDTH + 3 * ML_D_INNER + 2 * ML_HEADS
N_SSD = (DEPTH + 1) // 2
N_ML = DEPTH // 2

kernel_name = "hybrid_ssd_mlstm_adaln_trunk"

F32 = jnp.float32


def rmsnorm(u, w):
    u32 = u.astype(F32)
    y = u32 * lax.rsqrt(jnp.mean(u32 * u32, axis=-1, keepdims=True) + EPS) * w.astype(F32)
    return y.astype(u.dtype)


def causal_conv(u, w, b):
    ch = u.shape[-1]
    y = lax.conv_general_dilated(
        u, w[:, None, :].astype(u.dtype), window_strides=(1,),
        padding=[(CONV_WIDTH - 1, 0)], dimension_numbers=("NWC", "WIO", "NWC"),
        feature_group_count=ch)
    return y + b.astype(u.dtype)


def to_chunks(t, chunk):
    b, n = t.shape[0], t.shape[1]
    t = t.reshape((b, n // chunk, chunk) + t.shape[2:])
    return jnp.moveaxis(t, 1, 0)


def from_chunks(t):
    t = jnp.moveaxis(t, 0, 1)
    return t.reshape((t.shape[0], -1) + t.shape[3:])


def ssd_chunked_scan(xdt, a, bmat, cmat):
    bsz = xdt.shape[0]
    causal = jnp.tril(jnp.ones((SSD_CHUNK, SSD_CHUNK), dtype=bool))[None, :, :, None, None]

    def step(state, inp):
        xc, ac, bc, cc = inp
        acs = jnp.cumsum(ac, axis=1)
        seg = acs[:, :, None] - acs[:, None, :]
        decay = jnp.exp(jnp.where(causal, seg, -jnp.inf))
        cb = jnp.einsum("btgn,bsgn->btsg", cc, bc)
        y_diag = jnp.einsum("btsgr,bsgrp->btgrp", cb[..., None] * decay, xc)
        y_off = jnp.einsum("btgn,bgrpn->btgrp", cc, state) * jnp.exp(acs)[..., None]
        total = acs[:, -1]
        w_end = jnp.exp(total[:, None] - acs)
        new_state = state * jnp.exp(total)[..., None, None] + jnp.einsum(
            "bsgn,bsgr,bsgrp->bgrpn", bc, w_end, xc)
        return new_state, y_diag + y_off

    state0 = jnp.zeros((bsz, SSD_GROUPS, SSD_HEADS_PER_GROUP, SSD_HEAD_DIM, SSD_D_STATE), F32)
    _, y = lax.scan(step, state0, (to_chunks(xdt, SSD_CHUNK), to_chunks(a, SSD_CHUNK),
                                   to_chunks(bmat, SSD_CHUNK), to_chunks(cmat, SSD_CHUNK)))
    return from_chunks(y)


def ssd_mixer(h, w_in, conv_w, conv_b, dt_bias, a_log, d_skip, norm_w, w_out):
    bsz, seqlen, _ = h.shape
    proj = h @ w_in
    z, xbc, dt = jnp.split(proj, [SSD_D_INNER, SSD_D_INNER + SSD_CONV_DIM], axis=-1)
    xbc = jax.nn.silu(causal_conv(xbc, conv_w, conv_b)).astype(F32)
    xs, bmat, cmat = jnp.split(xbc, [SSD_D_INNER, SSD_D_INNER + SSD_BC_WIDTH], axis=-1)
    xs = xs.reshape(bsz, seqlen, SSD_GROUPS, SSD_HEADS_PER_GROUP, SSD_HEAD_DIM)
    bmat = bmat.reshape(bsz, seqlen, SSD_GROUPS, SSD_D_STATE)
    cmat = cmat.reshape(bsz, seqlen, SSD_GROUPS, SSD_D_STATE)
    dt = jax.nn.softplus(dt.astype(F32) + dt_bias.astype(F32))
    dt = dt.reshape(bsz, seqlen, SSD_GROUPS, SSD_HEADS_PER_GROUP)
    a = -jnp.exp(a_log.astype(F32)).reshape(SSD_GROUPS, SSD_HEADS_PER_GROUP) * dt
    y = ssd_chunked_scan(xs * dt[..., None], a, bmat, cmat)
    y = y + d_skip.astype(F32).reshape(SSD_GROUPS, SSD_HEADS_PER_GROUP, 1) * xs
    g = (y.reshape(bsz, seqlen, SSD_D_INNER) * jax.nn.silu(z.astype(F32)))
    g = g.reshape(bsz, seqlen, SSD_GROUPS, SSD_D_INNER // SSD_GROUPS)
    g = g * lax.rsqrt(jnp.mean(g * g, axis=-1, keepdims=True) + EPS)
    g = g.reshape(bsz, seqlen, SSD_D_INNER) * norm_w.astype(F32)
    return g.astype(h.dtype) @ w_out


def mlstm_chunked(q, k, v, i_pre, log_f):
    bsz = q.shape[0]
    causal = jnp.tril(jnp.ones((ML_CHUNK, ML_CHUNK), dtype=bool))[None, :, :, None]

    def step(carry, inp):
        c_st, n_st, m_st = carry
        qc, kc, vc, ic, fc = inp
        bcum = jnp.cumsum(fc, axis=1)
        dlog = bcum[:, :, None] - bcum[:, None, :] + ic[:, None, :]
        dlog = jnp.where(causal, dlog, -jnp.inf)
        inter = bcum + m_st[:, None]
        m_t = jnp.maximum(inter, jnp.max(dlog, axis=2))
        s = jnp.einsum("bthk,bshk->btsh", qc, kc) * jnp.exp(dlog - m_t[:, :, None])
        inter_w = jnp.exp(inter - m_t)
        num = jnp.einsum("btsh,bshv->bthv", s, vc) + inter_w[..., None] * jnp.einsum(
            "bthk,bhkv->bthv", qc, c_st)
        den = jnp.sum(s, axis=2) + inter_w * jnp.einsum("bthk,bhk->bth", qc, n_st)
        h_out = num / jnp.maximum(jnp.abs(den), jnp.exp(-m_t))[..., None]
        total = bcum[:, -1]
        wlog = total[:, None] - bcum + ic
        m_new = jnp.maximum(total + m_st, jnp.max(wlog, axis=1))
        w = jnp.exp(wlog - m_new[:, None])
        carry_w = jnp.exp(total + m_st - m_new)
        c_new = carry_w[..., None, None] * c_st + jnp.einsum("bsh,bshk,bshv->bhkv", w, kc, vc)
        n_new = carry_w[..., None] * n_st + jnp.einsum("bsh,bshk->bhk", w, kc)
        return (c_new, n_new, m_new), h_out

    carry0 = (jnp.zeros((bsz, ML_HEADS, ML_QK_DIM, ML_V_DIM), F32),
              jnp.zeros((bsz, ML_HEADS, ML_QK_DIM), F32),
              jnp.zeros((bsz, ML_HEADS), F32))
    _, h_out = lax.scan(step, carry0, (to_chunks(q, ML_CHUNK), to_chunks(k, ML_CHUNK),
                                       to_chunks(v, ML_CHUNK), to_chunks(i_pre, ML_CHUNK),
                                       to_chunks(log_f, ML_CHUNK)))
    return from_chunks(h_out)


def mlstm_mixer(h, w_in, conv_w, conv_b, igate_b, fgate_b, head_norm_w, w_out):
    bsz, seqlen, _ = h.shape
    proj = h @ w_in
    qk, v, o, z, gates = jnp.split(
        proj, [ML_QK_WIDTH, ML_QK_WIDTH + ML_D_INNER, ML_QK_WIDTH + 2 * ML_D_INNER,
               ML_QK_WIDTH + 3 * ML_D_INNER], axis=-1)
    qk = jax.nn.silu(causal_conv(qk, conv_w, conv_b)).astype(F32)
    q, k = jnp.split(qk, 2, axis=-1)
    q = q.reshape(bsz, seqlen, ML_HEADS, ML_QK_DIM)
    k = k.reshape(bsz, seqlen, ML_HEADS, ML_QK_DIM) * (ML_QK_DIM ** -0.5)
    v = v.astype(F32).reshape(bsz, seqlen, ML_HEADS, ML_V_DIM)
    i_pre, f_pre = jnp.split(gates.astype(F32), 2, axis=-1)
    i_pre = i_pre + igate_b.astype(F32)
    log_f = jax.nn.log_sigmoid(f_pre + fgate_b.astype(F32))
    h_tilde = mlstm_chunked(q, k, v, i_pre, log_f)
    o_gate = jax.nn.sigmoid(o.astype(F32)).reshape(bsz, seqlen, ML_HEADS, ML_V_DIM)
    hc = o_gate * h_tilde
    hc = hc * lax.rsqrt(jnp.mean(hc * hc, axis=-1, keepdims=True) + EPS)
    hc = hc * head_norm_w.astype(F32).reshape(ML_HEADS, ML_V_DIM)
    y = hc.reshape(bsz, seqlen, ML_D_INNER) * jax.nn.silu(z.astype(F32))
    return y.astype(h.dtype) @ w_out


def setup_inputs(seed: int = 0) -> dict:
    key = jax.random.key(seed)
    ks = jax.random.split(key, 24)
    nrm = jax.random.normal
    x = nrm(ks[0], (BATCH, SEQ, D_MODEL), F32)
    c = nrm(ks[1], (BATCH, D_MODEL), F32)
    norm_w = 1.0 + 0.02 * nrm(ks[2], (DEPTH, D_MODEL), F32)
    ada_w = 0.5 * D_MODEL ** -0.5 * nrm(ks[3], (DEPTH, D_MODEL, 3 * D_MODEL), F32)
    ada_b = 0.02 * nrm(ks[4], (DEPTH, 3 * D_MODEL), F32)
    ssd_w_in = D_MODEL ** -0.5 * nrm(ks[5], (N_SSD, D_MODEL, SSD_PROJ), F32)
    ssd_conv_w = CONV_WIDTH ** -0.5 * nrm(ks[6], (N_SSD, CONV_WIDTH, SSD_CONV_DIM), F32)
    ssd_conv_b = 0.02 * nrm(ks[7], (N_SSD, SSD_CONV_DIM), F32)
    u = jax.random.uniform(ks[8], (N_SSD, SSD_HEADS), F32)
    dt0 = jnp.exp(u * (np.log(0.1) - np.log(1e-3)).astype(np.float32) + np.float32(np.log(1e-3)))
    ssd_dt_bias = dt0 + jnp.log(-jnp.expm1(-dt0))
    ssd_a_log = jnp.log(jax.random.uniform(ks[9], (N_SSD, SSD_HEADS), F32, 1.0, 16.0))
    ssd_d = 1.0 + 0.02 * nrm(ks[10], (N_SSD, SSD_HEADS), F32)
    ssd_norm_w = 1.0 + 0.02 * nrm(ks[11], (N_SSD, SSD_D_INNER), F32)
    ssd_w_out = SSD_D_INNER ** -0.5 * nrm(ks[12], (N_SSD, SSD_D_INNER, D_MODEL), F32)
    ml_w_in = D_MODEL ** -0.5 * nrm(ks[13], (N_ML, D_MODEL, ML_PROJ), F32)
    ml_conv_w = CONV_WIDTH ** -0.5 * nrm(ks[14], (N_ML, CONV_WIDTH, ML_QK_WIDTH), F32)
    ml_conv_b = 0.02 * nrm(ks[15], (N_ML, ML_QK_WIDTH), F32)
    ml_igate_b = 0.1 * nrm(ks[16], (N_ML, ML_HEADS), F32)
    ml_fgate_b = 3.0 + 3.0 * jax.random.uniform(ks[17], (N_ML, ML_HEADS), F32)
    ml_norm_w = 1.0 + 0.02 * nrm(ks[18], (N_ML, ML_D_INNER), F32)
    ml_w_out = ML_D_INNER ** -0.5 * nrm(ks[19], (N_ML, ML_D_INNER, D_MODEL), F32)
    final_norm_w = 1.0 + 0.02 * nrm(ks[20], (D_MODEL,), F32)
    return {"x": x, "c": c, "norm_w": norm_w, "ada_w": ada_w, "ada_b": ada_b,
            "ssd_w_in": ssd_w_in, "ssd_conv_w": ssd_conv_w, "ssd_conv_b": ssd_conv_b,
            "ssd_dt_bias": ssd_dt_bias, "ssd_a_log": ssd_a_log, "ssd_d": ssd_d,
            "ssd_norm_w": ssd_norm_w, "ssd_w_out": ssd_w_out,
            "ml_w_in": ml_w_in, "ml_conv_w": ml_conv_w, "ml_conv_b": ml_conv_b,
            "ml_igate_b": ml_igate_b, "ml_fgate_b": ml_fgate_b, "ml_norm_w": ml_norm_w,
            "ml_w_out": ml_w_out, "final_norm_w": final_norm_w}


def reference(x, c, norm_w, ada_w, ada_b, ssd_w_in, ssd_conv_w, ssd_conv_b, ssd_dt_bias,
              ssd_a_log, ssd_d, ssd_norm_w, ssd_w_out, ml_w_in, ml_conv_w, ml_conv_b,
              ml_igate_b, ml_fgate_b, ml_norm_w, ml_w_out, final_norm_w):
    cond = jax.nn.silu(c)
    for i in range(DEPTH):
        mod = cond @ ada_w[i] + ada_b[i]
        shift, scale, gate = jnp.split(mod[:, None, :], 3, axis=-1)
        h = rmsnorm(x, norm_w[i]) * (1.0 + scale) + shift
        j = i // N_MIXERS
        if i % N_MIXERS == 0:
            out = ssd_mixer(h, ssd_w_in[j], ssd_conv_w[j], ssd_conv_b[j], ssd_dt_bias[j],
                            ssd_a_log[j], ssd_d[j], ssd_norm_w[j], ssd_w_out[j])
        else:
            out = mlstm_mixer(h, ml_w_in[j], ml_conv_w[j], ml_conv_b[j], ml_igate_b[j],
                              ml_fgate_b[j], ml_norm_w[j], ml_w_out[j])
        x = x + gate * out
    return rmsnorm(x, final_norm_w)
```

```python
from contextlib import ExitStack
import numpy as np
import concourse.bass as bass
import concourse.mybir as mybir
from concourse.bass_utils import run_bass_kernel_spmd

F32 = mybir.dt.float32
BF16 = mybir.dt.bfloat16
AF = mybir.ActivationFunctionType
ALU = mybir.AluOpType
AX = mybir.AxisListType


class Buf:
    __slots__ = ("name", "writers", "readers", "sem", "dcnt", "excl")

    def __init__(self, name, excl=False):
        self.name = name
        self.excl = excl
        self.writers = []
        self.readers = []
        self.sem = None
        self.dcnt = 0


class V:
    __slots__ = ("ap", "bufs")

    def __init__(self, ap, bufs):
        self.ap = ap
        self.bufs = tuple(bufs)

    def __getitem__(self, k):
        return V(self.ap[k], self.bufs)

    def re(self, s, **kw):
        return V(self.ap.rearrange(s, **kw), self.bufs)

    def bc(self, shape):
        return V(self.ap.to_broadcast(shape), self.bufs)


class Op:
    __slots__ = ("eng", "fn", "deps", "odeps", "signal", "val", "sem", "is_dma", "name", "inc", "cost", "lat", "epoch", "seq", "fin", "npred", "succ", "rt", "st", "crit", "line")


class Prog:
    ENGS = ["pe", "act", "dve", "pool", "sp"]
    BLK = {"pe": "tensor", "act": "scalar", "dve": "vector", "pool": "gpsimd", "sp": "sync"}

    def __init__(self, nc):
        self.nc = nc
        self.ops = {e: [] for e in self.ENGS}
        self.stack = ExitStack()
        self.dma_bufs = []
        self.nsb = 0
        self.all_dma_out = []
        self.pending_dma = []
        self.stacks = [self.stack]
        self.bar_scr = None
        self.epoch = 0
        self.nseq = 0
        self.debug_lines = False

    def sb(self, name, shape, dtype):
        self.nsb += 1
        t = self.stacks[-1].enter_context(self.nc.sbuf_tensor("%s_%d" % (name, self.nsb), list(shape), dtype))
        return V(t.ap(), [Buf(name)])

    def ps(self, name, shape, dtype=F32):
        t = self.stack.enter_context(self.nc.psum_tensor(name, list(shape), dtype))
        return V(t.ap(), [Buf(name, excl=True)])

    def dram(self, name, shape, dtype, kind="Internal"):
        t = self.nc.dram_tensor(name, list(shape), dtype, kind=kind)
        return V(t.ap(), [Buf(name)])

    def op(self, eng, fn, r=(), w=(), dma=False, key=None, name=None, extra=(), cost=100.0, lat=0.0, inc=16):
        op = Op()
        op.inc = inc
        op.eng, op.fn, op.signal, op.val, op.sem, op.is_dma, op.name = eng, fn, False, 0, None, dma, name
        op.deps = []
        op.odeps = []
        op.cost, op.lat, op.epoch = cost, lat, self.epoch
        self.nseq += 1
        op.seq = self.nseq
        op.crit = None
        if self.debug_lines:
            import sys as _s
            f = _s._getframe(1)
            while f.f_code.co_name in ("op", "dma", "mm", "tt", "ts", "stt", "act", "cp", "tr", "store", "rstd", "rowload", "allgather"):
                f = f.f_back
            op.line = f.f_lineno
        else:
            op.line = 0
        rb = [b for v in r for b in v.bufs]
        wb = [b for v in w for b in v.bufs]
        cand = []
        for b in rb:
            cand += b.writers
            if b.excl:
                cand += [x for x in b.readers if x.eng != eng]
        for b in wb:
            cand += b.writers
            cand += b.readers
        cand += list(extra)
        seen = set()
        for d in cand:
            if d is op or id(d) in seen:
                continue
            seen.add(id(d))
            if (not d.is_dma) and (not dma) and d.eng == eng and eng == "pe":
                op.odeps.append(d)
                continue
            d.signal = True
            op.deps.append(d)
        for b in wb:
            if b.readers:
                b.writers = [op]
                b.readers = []
            else:
                b.writers = [x for x in b.writers if not (x.eng == eng and x.is_dma == dma and not dma)] + [op]
        for b in rb:
            b.readers.append(op)
        if dma:
            kb = key if key is not None else (wb[0] if wb else rb[0])
            if kb.sem is None:
                self.dma_bufs.append(kb)
                kb.sem = "pending"
            kb.dcnt += 1
            op.sem = kb
            op.val = inc * kb.dcnt
            op.signal = True
            if inc == 16:
                self.pending_dma.append(op)
        self.ops[eng].append(op)
        return op

    def barrier(self):
        if self.bar_scr is None:
            self.bar_scr = {e: self.sb("bar_" + e, [128, 2], F32) for e in ("act", "dve", "pool")}
            for e in ("dve", "pool"):
                self.op(e, lambda g, a=self.bar_scr[e].ap: g.memset(a, 0.0), w=[self.bar_scr[e]])
            self.op("act", lambda g, a=self.bar_scr["act"].ap, b=self.bar_scr["dve"].ap: g.copy(a, b),
                    r=[self.bar_scr["dve"]], w=[self.bar_scr["act"]])
        self.epoch += 1
        arr = []
        for e in ("dve", "pool"):
            arr.append(self.op(e, lambda g, a=self.bar_scr[e].ap: g.memset(a[:, 0:1], 0.0), w=[self.bar_scr[e]]))
        arr.append(self.op("act", lambda g, a=self.bar_scr["act"].ap: g.copy(a[:, 0:1], a[:, 1:2]),
                           w=[self.bar_scr["act"]]))
        pend = list(self.pending_dma)
        self.pending_dma = []
        for e in self.ENGS:
            self.op(e, lambda g: g.nop(), extra=arr + pend, name="bar")
        self.epoch += 1

    def scope(self):
        return _Scope(self)

    def dma(self, eng, out, in_, key=None, nbytes=None, **kw):
        if nbytes is None:
            n = 1
            for d in out.ap.shape:
                n *= d
            nbytes = n * 4
        o = self.op(eng, lambda e: e.dma_start(out=out.ap, in_=in_.ap, **kw), r=[in_], w=[out], dma=True,
                    key=key, cost=(60.0 if eng == "sp" else 300.0 + nbytes / 2000.0), lat=2000.0 + nbytes / 100.0)
        return o

    def schedule(self):
        import heapq
        allops = [o for e in self.ENGS for o in self.ops[e]]
        allops.sort(key=lambda o: o.seq)
        for o in allops:
            o.succ = []
            o.npred = 0
            o.rt = 0.0
            o.fin = 0.0
        for o in allops:
            for d in o.deps + o.odeps:
                d.succ.append(o)
                o.npred += 1
        prio = {}
        for o in reversed(allops):
            m = 0.0
            for sx in o.succ:
                if sx.epoch == o.epoch:
                    p = prio[id(sx)]
                    if p > m:
                        m = p
            prio[id(o)] = o.cost + o.lat + m
        byep = {}
        for o in allops:
            byep.setdefault(o.epoch, []).append(o)
        tfree = {e: 0.0 for e in self.ENGS}
        busy = {e: 0.0 for e in self.ENGS}
        new = {e: [] for e in self.ENGS}
        tglob = 0.0
        for ep in sorted(byep):
            ops = byep[ep]
            wait = {e: [] for e in self.ENGS}
            ready = {e: [] for e in self.ENGS}
            left = len(ops)
            for o in ops:
                if o.npred == 0:
                    heapq.heappush(wait[o.eng], (o.rt, o.seq, o))
            while left:
                best = None
                for e in self.ENGS:
                    w, r = wait[e], ready[e]
                    while w and w[0][0] <= tfree[e]:
                        _, sq, o2 = heapq.heappop(w)
                        heapq.heappush(r, (-prio[id(o2)], sq, o2))
                    if r:
                        st = tfree[e]
                    elif w:
                        st = w[0][0]
                    else:
                        continue
                    if best is None or st < best[0]:
                        best = (st, e)
                assert best is not None, "scheduler deadlock (cross-epoch dependency?)"
                st, e = best
                if ready[e]:
                    _, _, o = heapq.heappop(ready[e])
                else:
                    _, _, o = heapq.heappop(wait[e])
                tfree[e] = st + o.cost
                o.st = st
                busy[e] += o.cost
                o.fin = st + o.cost + o.lat
                new[e].append(o)
                left -= 1
                for sx in o.succ:
                    lat = 60.0 if (sx.eng == o.eng and not o.is_dma) else 250.0
                    if sx.eng == "pe" and o.eng == "pe":
                        lat = 0.0
                    if o.fin + lat > sx.rt:
                        sx.rt = o.fin + lat
                        sx.crit = o
                    sx.npred -= 1
                    if sx.npred == 0 and sx.epoch == ep:
                        heapq.heappush(wait[sx.eng], (sx.rt, sx.seq, sx))
            t_prev = tglob
            tmax = max([tfree[e] for e in self.ENGS] + [o.fin for o in ops])
            tglob = tmax
            if len(ops) > 50:
                eb = {}
                for o in ops:
                    eb[o.eng] = eb.get(o.eng, 0.0) + o.cost
                print("  epoch %d: %.3f ms  n=%d  busy=%s" % (ep, (tmax - t_prev) / 1e6, len(ops),
                      {e: round(v / 1e6, 2) for e, v in eb.items()}))
            for e in self.ENGS:
                tfree[e] = tmax
        self.ops = new
        self.sim_time = max(tfree.values())
        self.sim_busy = busy

    def emit(self, final_wait_ops=(), sched=True):
        nc = self.nc
        st = self.stack
        if sched:
            self.schedule()
        esem = {e: st.enter_context(nc.semaphore("s_" + e)) for e in self.ENGS}
        for i, b in enumerate(self.dma_bufs):
            b.sem = st.enter_context(nc.semaphore("d%d" % i))
        for e in self.ENGS:
            c = 0
            for op in self.ops[e]:
                if op.is_dma:
                    op.sem = op.sem.sem if isinstance(op.sem, Buf) else op.sem
                elif op.signal:
                    c += 1
                    op.val = c
                    op.sem = esem[e]
        self.counts = {e: sum(1 for o in self.ops[e]) for e in self.ENGS}
        block = st.enter_context(nc.Block())
        for e in self.ENGS:
            ops = self.ops[e]
            if not ops:
                continue

            def body(eng, ops=ops):
                waited = {}
                for op in ops:
                    need = {}
                    for d in op.deps:
                        k = id(d.sem)
                        if k not in need or need[k][1] < d.val:
                            need[k] = (d.sem, d.val)
                    for k, (s, v) in need.items():
                        if waited.get(k, 0) < v:
                            eng.wait_ge(s, v)
                            waited[k] = v
                    ins = op.fn(eng)
                    if op.signal:
                        ins.then_inc(op.sem, op.inc if op.is_dma else 1)

            getattr(block, self.BLK[e])(body)

    def close(self):
        self.stack.close()


class _Scope:
    def __init__(self, P):
        self.P = P

    def __enter__(self):
        self.st = ExitStack()
        self.P.stacks.append(self.st)
        return self

    def __exit__(self, *a):
        self.P.barrier()
        self.P.stacks.pop()
        self.st.close()
        return False


def gap_report(P, epoch, eng="pe", top=12):
    ops = [o for o in P.ops[eng] if o.epoch == epoch]
    agg = {}
    prev_end = ops[0].st
    for o in ops:
        gap = o.st - prev_end
        if gap > 1.0:
            c = o.crit
            key = (o.line, c.eng if c else None, c.line if c else None, c.is_dma if c else None)
            a = agg.setdefault(key, [0.0, 0])
            a[0] += gap
            a[1] += 1
        prev_end = o.st + o.cost
    tot = sum(a[0] for a in agg.values())
    print("gap report epoch %d eng %s: total gap %.3f ms" % (epoch, eng, tot / 1e6))
    for key, a in sorted(agg.items(), key=lambda kv: -kv[1][0])[:top]:
        print("   op@%s waits on %s@%s dma=%s : %.3f ms (%d)" % (key[0], key[1], key[2], key[3], a[0] / 1e6, a[1]))


D = 2048
KC = 16
EPS = 1e-6
NEG = -30000.0


class Cut(Exception):
    pass


class KB:
    cut = None
    sched = True

    def ck(self, n):
        if self.cut is not None and self.cut == n:
            raise Cut()

    def __init__(self, T, nlayers=2):
        self.T = T
        self.NSC = T // 512
        self.NST = T // 128
        nc = bass.Bass("TRN2", target_bir_lowering=False)
        self.nc = nc
        P = Prog(nc)
        self.P = P
        ei = lambda n, s, d=F32: P.dram(n, s, d, kind="ExternalInput")
        self.x = ei("x", [T, D])
        self.xh = ei("x_half", [T, 1024])
        self.cT = ei("cT", [128, 16])
        self.norm_w = ei("norm_w", [2, D])
        self.fnorm_w = ei("final_norm_w", [1, D])
        self.ada_w = ei("ada_w", [2, D, 5120])
        self.ada_b = ei("ada_b", [2, 5120])
        self.s_win = ei("ssd_w_in", [D, 5152])
        self.s_wout = ei("ssd_w_out", [4096, 1024])
        self.s_cw = ei("ssd_cw", [128, 24, 4])
        self.s_cb = ei("ssd_cb", [128, 24])
        self.s_dtb = ei("ssd_dt_bias", [1, 32])
        self.s_alog = ei("ssd_a_log", [1, 32])
        self.s_d = ei("ssd_d", [1, 32])
        self.s_nw = ei("ssd_norm_w", [1, 2048])
        self.m_win = ei("ml_w_in", [D, 8200])
        self.m_wout = ei("ml_w_out", [4096, 1024])
        self.m_cw = ei("ml_cw", [128, 16, 4])
        self.m_cb = ei("ml_cb", [128, 16])
        self.m_ib = ei("ml_igate_b", [1, 4])
        self.m_fb = ei("ml_fgate_b", [1, 4])
        self.m_nw = ei("ml_norm_w", [1, 2048])
        self.consts = ei("consts", [128, 4, 128])
        self.out = P.nc.dram_tensor("out", [T, D], F32, kind="ExternalOutput").ap()
        self.out_bufs = [Buf("out%d" % i) for i in range(self.NST)]
        self.hT_d = nc.dram_tensor("hT_d", [self.NSC, 128, 16, 512], BF16).ap()
        self.hT_bufs = [Buf("hT%d" % i) for i in range(self.NSC)]
        R = self.NST * 128
        self.YC = max(1, R // 2048)
        self.RY = R // self.YC
        self.NY = self.NST // self.YC
        self.yT_loc = nc.dram_tensor("yT_loc", [4, self.YC, self.RY, 512], BF16).ap()
        self.yT_loc_bufs = [[Buf("yTl%d_%d" % (i, j)) for j in range(self.YC)] for i in range(4)]
        self.yT_all = nc.dram_tensor("yT_all", [4, self.YC, 2 * self.RY, 512], BF16).ap()
        self.yT_all_bufs = [[Buf("yTa%d_%d" % (i, j)) for j in range(self.YC)] for i in range(4)]
        self.NXC = max(4, T // 512)
        self.NQ = self.NST // self.NXC
        TQ = T // self.NXC
        self.xl, self.xa, self.xl_bufs, self.xa_bufs = {}, {}, {}, {}
        for l in (1, 2):
            self.xl[l] = nc.dram_tensor("x%d_loc" % l, [self.NXC, TQ, 1024], F32).ap()
            self.xa[l] = nc.dram_tensor("x%d_all" % l, [self.NXC, 2 * TQ, 1024], F32).ap()
            self.xl_bufs[l] = [Buf("x%dl%d" % (l, i)) for i in range(self.NXC)]
            self.xa_bufs[l] = [Buf("x%da%d" % (l, i)) for i in range(self.NXC)]
        self.cc_key = Buf("cc")
        self.xin_bufs = [self.x.bufs[0]] * self.NST
        self.pb = [P.ps("pb%d" % i, [128, 512], F32) for i in range(7)]
        self.pT = P.ps("pT", [128, 1024], BF16)
        self.pT2 = V(self.pb[0].ap.bitcast(BF16), self.pb[0].bufs)
        self.setup_consts()
        P.barrier()

    @staticmethod
    def fsz(v):
        n = 1
        for d in v.ap.shape[1:]:
            n *= d
        return n

    def ecost(self, eng, out, *ins):
        F = self.fsz(out)
        if eng == "dve":
            return 90.0 + F / 0.8
        if eng == "act":
            return 220.0 + F / 0.9
        return 120.0 + F / 0.5

    def mm(self, out, lhsT, rhs, start=True, stop=True):
        N = self.fsz(rhs)
        f32 = lhsT.ap.dtype == F32
        c = max(N, 64) / 2.4 * (4 if f32 else 1) + (180.0 if f32 else 90.0) + 10.0
        self.P.op("pe", lambda e: e.matmul(out.ap, lhsT.ap, rhs.ap, start=start, stop=stop), r=[lhsT, rhs], w=[out], cost=c)

    def tr(self, out, in_):
        idb = self.identb
        self.P.op("pe", lambda e: e.transpose(out.ap, in_.ap, idb.ap), r=[in_, idb], w=[out], cost=155.0)

    def tt(self, eng, out, in0, in1, op):
        self.P.op(eng, lambda e: e.tensor_tensor(out.ap, in0.ap, in1.ap, op), r=[in0, in1], w=[out],
                  cost=self.ecost(eng, out))

    def ts(self, eng, out, in0, s1, s2, op0, op1=None):
        r = [in0] + [s for s in (s1, s2) if isinstance(s, V)]
        a1 = s1.ap if isinstance(s1, V) else s1
        a2 = s2.ap if isinstance(s2, V) else s2
        c = self.ecost(eng, out)
        if op1 is None:
            self.P.op(eng, lambda e: e.tensor_single_scalar(out.ap, in0.ap, a1, op0), r=r, w=[out], cost=c)
        else:
            self.P.op(eng, lambda e: e.tensor_scalar(out.ap, in0.ap, a1, a2, op0, op1), r=r, w=[out], cost=c)

    def stt(self, eng, out, in0, sc, in1, op0, op1):
        r = [in0, in1] + ([sc] if isinstance(sc, V) else [])
        a = sc.ap if isinstance(sc, V) else sc
        self.P.op(eng, lambda e: e.scalar_tensor_tensor(out.ap, in0.ap, a, in1.ap, op0, op1), r=r, w=[out],
                  cost=self.ecost(eng, out))

    def act(self, out, in_, func, bias=None, scale=None, accum=None):
        r = [in_] + [s for s in (bias, scale) if isinstance(s, V)]
        w = [out] + ([accum] if accum is not None else [])
        kw = {}
        if bias is not None:
            kw["bias"] = bias.ap if isinstance(bias, V) else bias
        if scale is not None:
            kw["scale"] = scale.ap if isinstance(scale, V) else scale
        if accum is not None:
            kw["accum_out"] = accum.ap
        self.P.op("act", lambda e: e.activation(out.ap, in_.ap, func, **kw), r=r, w=w, cost=self.ecost("act", out))

    def cp(self, eng, out, in_):
        c = self.ecost(eng, out)
        if eng == "act":
            self.P.op("act", lambda e: e.copy(out.ap, in_.ap), r=[in_], w=[out], cost=c)
        else:
            self.P.op(eng, lambda e: e.tensor_copy(out.ap, in_.ap), r=[in_], w=[out], cost=c)

    def allgather(self, src_ap, src_buf, dst_ap, dst_buf, nbytes):
        rg = [[0, 1], [2, 3], [4, 5], [6, 7]]
        self.P.op("pool", lambda e: e.collective_compute("AllGather", ALU.bypass, replica_groups=rg,
                                                         ins=[src_ap.opt()], outs=[dst_ap.opt()]),
                  r=[V(src_ap, [src_buf])], w=[V(dst_ap, [dst_buf])], dma=True, inc=1, key=self.cc_key,
                  cost=2000.0, lat=30000.0 + nbytes / 60.0)

    def src_x(self, i):
        return V(self.x.ap[i * 128:(i + 1) * 128, :].rearrange("p (r n) -> p r n", r=2), self.x.bufs)

    def src_all(self, l):
        def f(i):
            q, o = i // self.NQ, (i % self.NQ) * 128
            ap = self.xa[l][q].rearrange("(r t) n -> t r n", r=2)[o:o + 128]
            return V(ap, [self.xa_bufs[l][q]])
        return f

    def store(self, dst, src):
        self.P.dma("sp", dst, src, key=src.bufs[0])

    def rowload(self, name, src_ap, n, src_v):
        t = self.P.sb(name, [128, n], F32)
        self.P.dma("sp", t, V(src_ap.partition_broadcast(128), src_v.bufs))
        return t

    def rstd(self, ss, n):
        self.ts("dve", ss, ss, 1.0 / n, EPS, ALU.mult, ALU.add)
        self.act(ss, ss, AF.Sqrt)
        self.P.op("dve", lambda e: e.reciprocal(ss.ap, ss.ap), r=[ss], w=[ss])

    def setup_consts(self):
        P = self.P
        cf = P.sb("cf", [128, 4, 128], F32)
        P.dma("sp", cf, self.consts)
        self.ident_f = cf[:, 0, :]
        self.tri = cf[:, 1, :]
        self.ones_f = cf[:, 3, :]
        self.identb = P.sb("identb", [128, 128], BF16)
        self.cp("dve", self.identb, cf[:, 0, :])
        self.negm4 = P.sb("negm4", [128, 4, 128], BF16)
        self.cp("dve", self.negm4, V(cf.ap[:, 2, :].unsqueeze(1).to_broadcast([128, 4, 128]), cf.bufs))
        self.ones_b = P.sb("ones_b", [128, 1], BF16)
        self.cp("dve", self.ones_b, cf[:, 3, 0:1])

    def phase0(self, l, mod):
        with self.P.scope():
            self.phase0_body(l, mod, 0)

    def phase0_body(self, l, mod, pbase):
        P = self.P
        if True:
            cs = P.sb("cs", [128, 16], F32)
            P.dma("sp", cs, self.cT)
            self.act(cs, cs, AF.Silu)
            condB = P.sb("condB", [128, 16, 128], BF16)
            self.cp("dve", condB, V(cs.ap.unsqueeze(2).to_broadcast([128, 16, 128]), cs.bufs))
            adab = self.rowload("adab", self.ada_b.ap[l:l + 1, :], 5120, self.ada_b)
            nw = self.rowload("nw", self.norm_w.ap[l:l + 1, :], D, self.norm_w)
            wv = self.ada_w.ap[l].rearrange("(p k) n -> p k n", k=16)
            slabs = [P.sb("adaw%d" % i, [128, 16, 512], BF16) for i in range(2)]
            for n in range(10):
                sl = slabs[n % 2]
                P.dma("pool", sl, V(wv[:, :, n * 512:(n + 1) * 512], self.ada_w.bufs))
                ps = self.pb[pbase + n % 2]
                for k in range(16):
                    self.mm(ps, condB[:, k, :], sl[:, k, :], start=(k == 0), stop=(k == 15))
                self.tt("dve", mod[:, n * 512:(n + 1) * 512], ps, adab[:, n * 512:(n + 1) * 512], ALU.add)
            self.stt("dve", mod[:, D:2 * D], mod[:, D:2 * D], 1.0, nw, ALU.add, ALU.mult)

    def phaseA(self, srcf, mod, phase0_l=None):
        P = self.P
        with P.scope():
            if phase0_l is not None:
                self.phase0_body(phase0_l, mod, 0)
            xt = [P.sb("xt%d" % i, [128, D], F32) for i in range(3)]
            hn_ = [P.sb("hn%d" % i, [128, D], F32) for i in range(2)]
            hb_ = [P.sb("hb%d" % i, [128, D], BF16) for i in range(2)]
            junk_ = [P.sb("junkA%d" % i, [128, D], BF16) for i in range(2)]
            ss = [P.sb("ssA%d" % i, [128, 1], F32) for i in range(2)]
            stage = [P.sb("hTst%d" % i, [128, 16, 512], BF16) for i in range(2)]
            for i in range(self.NST):
                sc, c = i // 4, i % 4
                x_ = xt[i % 3]
                hn, hb, junk = hn_[i % 2], hb_[i % 2], junk_[i % 2]
                P.dma("sp", x_.re("p (r n) -> p r n", r=2), srcf(i))
                s_ = ss[i % 2]
                self.act(junk, x_, AF.Square, accum=s_)
                self.rstd(s_, D)
                self.stt("dve", hn, x_, s_, mod[:, D:2 * D], ALU.mult, ALU.mult)
                self.tt("pool", hb, hn, mod[:, 0:D], ALU.add)
                st = stage[sc % 2]
                for half in range(2):
                    for kk in range(8):
                        k = half * 8 + kk
                        self.tr(self.pT[:, kk * 128:(kk + 1) * 128], hb[:, k * 128:(k + 1) * 128])
                    self.cp("act" if half == 0 else "dve", st[:, half * 8:(half + 1) * 8, c * 128:(c + 1) * 128],
                            self.pT.re("p (k t) -> p k t", k=8))
                if c == 3:
                    self.store(V(self.hT_d[sc], [self.hT_bufs[sc]]), st)

    def conv_fm(self, hs, wlist, U, XB, cw, cb, cidx, first):
        P = self.P
        nj = len(wlist)
        for j in range(nj):
            ps = self.pb[j % 2]
            for k in range(16):
                self.mm(ps, wlist[j][:, k, :], hs[:, k, :], start=(k == 0), stop=(k == 15))
            Uj = U[j]
            if first:
                self.P.op("pool", lambda e, a=Uj.ap[:, 0:3]: e.memset(a, 0.0), w=[Uj])
            self.cp("act", Uj[:, 3:515], ps)
            acc = self.cacc[j % 2]
            ci = cidx[j]
            self.ts("dve", acc, Uj[:, 0:512], cw[:, ci, 0:1], cb[:, ci:ci + 1], ALU.mult, ALU.add)
            for k in range(1, 4):
                self.stt("dve", acc, Uj[:, k:k + 512], cw[:, ci, k:k + 1], acc, ALU.mult, ALU.add)
            self.act(XB[:, j, :], acc, AF.Silu)
            self.cp("pool", Uj[:, 0:3], Uj[:, 512:515])

    def store_yT(self, GN, i, cbase):
        P = self.P
        for q in range(4):
            self.tr(self.pT2[:, q * 128:(q + 1) * 128], GN[:, q * 128:(q + 1) * 128])
        ys = self.yst[i % 2]
        self.cp("act", ys, self.pT2[:, 0:512].re("p (q t) -> p q t", q=4))
        gl = cbase
        pt, off = i // self.NY, (i % self.NY) * 128
        self.store(V(self.yT_loc[gl, pt, off:off + 128, :], [self.yT_loc_bufs[gl][pt]]), ys.re("p q t -> p (q t)"))
        if i % self.NY == self.NY - 1:
            self.allgather(self.yT_loc[gl, pt], self.yT_loc_bufs[gl][pt], self.yT_all[gl, pt], self.yT_all_bufs[gl][pt],
                           2 * self.RY * 512 * 2)

    def phaseB_ssd(self):
        try:
            self.phaseB_ssd_()
        except Cut:
            self.P.stacks.pop()
            self.P.barrier()

    def phaseB_ssd_(self):
        P = self.P
        T = self.T
        with P.scope():
            wv = self.s_win.ap.rearrange("(k p) n -> p k n", p=128)
            Wz = [P.sb("Wz%d" % i, [128, 16, 512], BF16) for i in range(2)]
            Wx = [P.sb("Wx%d" % i, [128, 16, 512], BF16) for i in range(2)]
            Wbc = [P.sb("Wbc%d" % i, [128, 16, 256], BF16) for i in range(2)]
            Wdt = [P.sb("Wdt%d" % i, [128, 16, 8], BF16) for i in range(2)]
            hTs = [P.sb("hTs%d" % i, [128, 16, 512], BF16) for i in range(2)]
            cw = P.sb("cw", [128, 24, 4], F32)
            cb = P.sb("cb", [128, 24], F32)
            P.dma("sp", cw, self.s_cw)
            P.dma("sp", cb, self.s_cb)
            dtb = self.rowload("dtb", self.s_dtb.ap, 32, self.s_dtb)
            aneg = self.rowload("aneg", self.s_alog.ap, 32, self.s_alog)
            self.act(aneg, aneg, AF.Exp)
            self.ts("dve", aneg, aneg, -1.0, None, ALU.mult)
            dsk = self.rowload("dsk", self.s_d.ap, 32, self.s_d)
            nwr = P.sb("nwr", [128, 512], F32)
            U = [P.sb("U%d" % j, [128, 515], F32) for j in range(6)]
            self.cacc = [P.sb("cacc%d" % j, [128, 512], F32) for j in range(2)]
            XB_ = [P.sb("XB%d" % q_, [128, 6, 512], BF16) for q_ in range(2)]
            SZ_ = [P.sb("SZ%d" % q_, [128, 4, 512], BF16) for q_ in range(2)]
            DT_ = [P.sb("DT%d" % q_, [128, 4, 8], F32) for q_ in range(2)]
            Aa_ = [P.sb("Aa%d" % q_, [128, 4, 8], F32) for q_ in range(2)]
            XT_ = [P.sb("XT%d" % q_, [128, 512], BF16) for q_ in range(2)]
            BT_ = [P.sb("BT%d" % q_, [128, 128], BF16) for q_ in range(2)]
            cbm_ = [P.sb("cbm%d" % q_, [128, 128], F32) for q_ in range(2)]
            arhs_ = [P.sb("arhs", [128, 8, 128], F32)] * 2
            acs_ = [P.sb("acs%d" % q_, [128, 8], F32) for q_ in range(2)]
            eacs_ = [P.sb("eacs%d" % q_, [128, 8], F32) for q_ in range(2)]
            etot_ = [P.sb("etot%d" % q_, [128, 8], F32) for q_ in range(2)]
            wend_ = [P.sb("wend%d" % q_, [128, 8], F32) for q_ in range(2)]
            segs_ = [P.sb("segs", [128, 4, 128], F32)] * 2
            MT_ = [P.sb("MT%d" % q_, [128, 8, 128], BF16) for q_ in range(2)]
            XDT_ = [P.sb("XDT%d" % q_, [128, 512], BF16) for q_ in range(2)]
            XW_ = [P.sb("XW%d" % q_, [128, 512], BF16) for q_ in range(2)]
            S = P.sb("S", [128, 512], F32)
            Sb = P.sb("Sb", [128, 512], BF16)
            t1_ = [P.sb("t1%d" % q_, [128, 512], F32) for q_ in range(2)]
            t2_ = [P.sb("t2", [128, 512], F32)] * 2
            ssq_ = [P.sb("ssq%d" % q_, [128, 1], F32) for q_ in range(2)]
            GN_ = [P.sb("GN%d" % q_, [128, 512], BF16) for q_ in range(2)]
            self.yst = [P.sb("yst%d" % i, [128, 4, 128], BF16) for i in range(2)]
            pmisc, pseg, pyd, pyo, pds, pz = self.pb[3], self.pb[4], self.pb[5], self.pb[6], self.pb[2], self.pb[2]

            def loadW(g):
                b = g % 2
                o = g * 1288
                P.dma("pool", Wz[b], V(wv[:, :, o:o + 512], self.s_win.bufs))
                P.dma("pool", Wx[b], V(wv[:, :, o + 512:o + 1024], self.s_win.bufs))
                P.dma("pool", Wbc[b], V(wv[:, :, o + 1024:o + 1280], self.s_win.bufs))
                P.dma("pool", Wdt[b], V(wv[:, :, o + 1280:o + 1288], self.s_win.bufs))

            loadW(0)
            for g in range(4 if self.cut is None else 1):
                b = g % 2
                if g + 1 < 4:
                    loadW(g + 1)
                P.op("pool", lambda e: e.memset(S.ap, 0.0), w=[S])
                P.op("pool", lambda e: e.memset(Sb.ap, 0.0), w=[Sb])
                P.dma("sp", nwr, V(self.s_nw.ap[:, g * 512:(g + 1) * 512].partition_broadcast(128), self.s_nw.bufs))
                wl = [Wx[b][:, :, j * 128:(j + 1) * 128] for j in range(4)] + [Wbc[b][:, :, 0:128], Wbc[b][:, :, 128:256]]
                cidx = [g * 4 + j for j in range(4)] + [16 + g, 20 + g]
                for sc in range(self.NSC):
                    hs = hTs[sc % 2]
                    XB = XB_[sc % 2]
                    SZ = SZ_[sc % 2]
                    DT = DT_[sc % 2]
                    Aa = Aa_[sc % 2]
                    P.dma("sp", hs, V(self.hT_d[sc], [self.hT_bufs[sc]]))
                    self.ck(1)
                    self.conv_fm(hs, wl, U, XB, cw, cb, cidx, sc == 0)
                    self.ck(2)
                    for c in range(4):
                        tok = slice(c * 128, (c + 1) * 128)
                        for k in range(16):
                            self.mm(pz, hs[:, k, tok], Wz[b][:, k, :], start=(k == 0), stop=(k == 15))
                        self.act(SZ[:, c, :], pz, AF.Silu)
                        for k in range(16):
                            self.mm(pmisc[:, 0:8], hs[:, k, tok], Wdt[b][:, k, :], start=(k == 0), stop=(k == 15))
                        self.tt("dve", DT[:, c, :], pmisc[:, 0:8], dtb[:, g * 8:(g + 1) * 8], ALU.add)
                        self.act(DT[:, c, :], DT[:, c, :], AF.Exp)
                        self.act(DT[:, c, :], DT[:, c, :], AF.Ln, bias=1.0)
                        self.tt("dve", Aa[:, c, :], DT[:, c, :], aneg[:, g * 8:(g + 1) * 8], ALU.mult)
                    self.ck(3)
                    for c in range(4):
                        i = sc * 4 + c
                        tok = slice(c * 128, (c + 1) * 128)
                        a = Aa[:, c, :]
                        XT = XT_[i % 2]
                        BT = BT_[i % 2]
                        cbm = cbm_[i % 2]
                        arhs = arhs_[i % 2]
                        acs = acs_[i % 2]
                        eacs = eacs_[i % 2]
                        etot = etot_[i % 2]
                        wend = wend_[i % 2]
                        segs = segs_[i % 2]
                        MT = MT_[i % 2]
                        XDT = XDT_[i % 2]
                        XW = XW_[i % 2]
                        t1 = t1_[i % 2]
                        t2 = t2_[i % 2]
                        ssq = ssq_[i % 2]
                        GN = GN_[i % 2]
                        junk = t2
                        for j in range(5):
                            self.tr(self.pT[:, j * 128:(j + 1) * 128], XB[:, j, tok])
                        self.cp("act", XT, self.pT[:, 0:512])
                        self.cp("dve", BT, self.pT[:, 512:640])
                        self.ck(4)
                        self.mm(pmisc[:, 128:256], XB[:, 4, tok], XB[:, 5, tok])
                        self.tt("dve", cbm, pmisc[:, 128:256], self.tri, ALU.mult)
                        self.tt("dve", arhs, V(self.tri.ap.unsqueeze(1).to_broadcast([128, 8, 128]), self.tri.bufs),
                                V(a.ap.unsqueeze(2).to_broadcast([128, 8, 128]), a.bufs), ALU.mult)
                        self.mm(pmisc[:, 8:16], self.tri, a)
                        self.mm(pmisc[:, 16:24], self.ones_f, a)
                        self.cp("act", acs, pmisc[:, 8:16])
                        self.act(eacs, pmisc[:, 8:16], AF.Exp)
                        self.act(etot, pmisc[:, 16:24], AF.Exp)
                        self.tt("dve", wend, pmisc[:, 16:24], acs, ALU.subtract)
                        self.act(wend, wend, AF.Exp)
                        self.ck(5)
                        for hh in range(2):
                            self.mm(pseg, self.ones_f, arhs[:, hh * 4:(hh + 1) * 4, :].re("p a b -> p (a b)"), start=True, stop=False)
                            self.mm(pseg, self.identb, self.negm4.re("p a b -> p (a b)"), start=False, stop=True)
                            self.tt("dve", segs, pseg.re("p (a b) -> p a b", a=4),
                                    V(acs.ap[:, hh * 4:(hh + 1) * 4].unsqueeze(2).to_broadcast([128, 4, 128]), acs.bufs), ALU.subtract)
                            self.act(segs, segs, AF.Exp)
                            self.tt("pool", MT[:, hh * 4:(hh + 1) * 4, :], segs,
                                    V(cbm.ap.unsqueeze(1).to_broadcast([128, 4, 128]), cbm.bufs), ALU.mult)
                        self.ck(6)
                        dtc = DT[:, c, :]
                        XT3 = XT.re("p (r q) -> p r q", r=8)
                        self.tt("dve", XDT.re("p (r q) -> p r q", r=8), XT3,
                                V(dtc.ap.unsqueeze(2).to_broadcast([128, 8, 64]), dtc.bufs), ALU.mult)
                        self.tt("pool", XW.re("p (r q) -> p r q", r=8), XDT.re("p (r q) -> p r q", r=8),
                                V(wend.ap.unsqueeze(2).to_broadcast([128, 8, 64]), wend.bufs), ALU.mult)
                        for r in range(8):
                            self.mm(pyd[:, r * 64:(r + 1) * 64], MT[:, r, :], XDT[:, r * 64:(r + 1) * 64])
                        self.mm(pyo, XB[:, 5, tok], Sb)
                        self.mm(pds, BT, XW)
                        self.ck(7)
                        self.tt("dve", t1.re("p (r q) -> p r q", r=8), pyo.re("p (r q) -> p r q", r=8),
                                V(eacs.ap.unsqueeze(2).to_broadcast([128, 8, 64]), eacs.bufs), ALU.mult)
                        self.tt("dve", t1, pyd, t1, ALU.add)
                        self.tt("pool", t2.re("p (r q) -> p r q", r=8), XT3,
                                V(dsk.ap[:, g * 8:(g + 1) * 8].unsqueeze(2).to_broadcast([128, 8, 64]), dsk.bufs), ALU.mult)
                        self.tt("pool", t1, t1, t2, ALU.add)
                        self.tt("dve", S.re("p (r q) -> p r q", r=8), S.re("p (r q) -> p r q", r=8),
                                V(etot.ap.unsqueeze(2).to_broadcast([128, 8, 64]), etot.bufs), ALU.mult)
                        self.tt("dve", S, pds, S, ALU.add)
                        self.cp("act", Sb, S)
                        self.ck(8)
                        self.tt("dve", t1, t1, SZ[:, c, :], ALU.mult)
                        self.act(junk, t1, AF.Square, accum=ssq)
                        self.rstd(ssq, 512)
                        self.stt("dve", GN, t1, ssq, nwr, ALU.mult, ALU.mult)
                        self.ck(9)
                        self.store_yT(GN, i, g)
                        self.ck(10)

    def phaseB_ml(self):
        P = self.P
        with P.scope():
            wv = self.m_win.ap.rearrange("(k p) n -> p k n", p=128)
            Wqk = P.sb("Wqk", [128, 16, 512], BF16)
            Wv = P.sb("Wv", [128, 16, 512], BF16)
            Wo = P.sb("Wo", [128, 16, 512], BF16)
            Wzz = P.sb("Wzz", [128, 16, 512], BF16)
            Wg = P.sb("Wg", [128, 16, 2], BF16)
            hTs = [P.sb("hTs%d" % i, [128, 16, 512], BF16) for i in range(2)]
            cw = P.sb("cw", [128, 16, 4], F32)
            cb = P.sb("cb", [128, 16], F32)
            P.dma("sp", cw, self.m_cw)
            P.dma("sp", cb, self.m_cb)
            ibr = self.rowload("ibr", self.m_ib.ap, 4, self.m_ib)
            self.ts("dve", ibr, ibr, float(np.log(1.0 / 16.0)), None, ALU.add)
            nfb = self.rowload("nfb", self.m_fb.ap, 4, self.m_fb)
            self.ts("dve", nfb, nfb, -1.0, None, ALU.mult)
            nwr = P.sb("nwr", [128, 512], F32)
            U = [P.sb("U%d" % j, [128, 515], F32) for j in range(4)]
            self.cacc = [P.sb("cacc%d" % j, [128, 512], F32) for j in range(2)]
            QK_ = [P.sb("QK%d" % q_, [128, 4, 512], BF16) for q_ in range(2)]
            Vv_ = [P.sb("Vv%d" % q_, [128, 4, 512], BF16) for q_ in range(2)]
            SO_ = [P.sb("SO", [128, 4, 512], F32)] * 2
            SZ_ = [P.sb("SZ", [128, 4, 512], F32)] * 2
            IG_ = [P.sb("IG%d" % q_, [128, 4], F32) for q_ in range(2)]
            Aa_ = [P.sb("Aa%d" % q_, [128, 4], F32) for q_ in range(2)]
            KT_ = [P.sb("KT%d" % q_, [128, 256], BF16) for q_ in range(2)]
            sm_ = [P.sb("sm%d" % q_, [128, 128], F32) for q_ in range(2)]
            arhs_ = [P.sb("arhs%d" % q_, [128, 128], F32) for q_ in range(2)]
            bcum_ = [P.sb("bcum%d" % q_, [128, 1], F32) for q_ in range(2)]
            ebc_ = [P.sb("ebc%d" % q_, [128, 1], F32) for q_ in range(2)]
            etot_ = [P.sb("etot%d" % q_, [128, 1], F32) for q_ in range(2)]
            w2_ = [P.sb("w2%d" % q_, [128, 1], F32) for q_ in range(2)]
            w2b_ = [P.sb("w2b%d" % q_, [128, 1], BF16) for q_ in range(2)]
            segs_ = [P.sb("segs%d" % q_, [128, 128], F32) for q_ in range(2)]
            PT_ = [P.sb("PT%d" % q_, [128, 128], BF16) for q_ in range(2)]
            VW_ = [P.sb("VW%d" % q_, [128, 512], BF16) for q_ in range(2)]
            C = P.sb("C", [128, 2, 512], F32)
            Cb = P.sb("Cb", [128, 2, 512], BF16)
            nst = P.sb("nst", [128, 2], F32)
            nb = P.sb("nb", [128, 2], BF16)
            den_ = [P.sb("den%d" % q_, [128, 1], F32) for q_ in range(2)]
            rinv_ = [P.sb("rinv%d" % q_, [128, 1], F32) for q_ in range(2)]
            c2_ = [P.sb("c2%d" % q_, [128, 1], F32) for q_ in range(2)]
            t1_ = [P.sb("t1%d" % q_, [128, 512], F32) for q_ in range(2)]
            t2_ = [P.sb("t2%d" % q_, [128, 512], F32) for q_ in range(2)]
            junk_ = [P.sb("junkB%d" % q_, [128, 512], F32) for q_ in range(2)]
            ssq_ = [P.sb("ssq%d" % q_, [128, 1], F32) for q_ in range(2)]
            GN_ = [P.sb("GN%d" % q_, [128, 512], BF16) for q_ in range(2)]
            self.yst = [P.sb("yst%d" % i, [128, 4, 128], BF16) for i in range(2)]
            pmisc, pyd, pyo = self.pb[3], self.pb[5], self.pb[6]
            ptms = [self.pb[2], self.pb[4]]
            pseg = self.pb[3][:, 256:384]
            pdc = [self.pb[0], self.pb[1]]
            for h in range(4):
                o = h * 2050
                P.dma("pool", Wqk, V(wv[:, :, o:o + 512], self.m_win.bufs))
                P.dma("pool", Wv, V(wv[:, :, o + 512:o + 1024], self.m_win.bufs))
                P.dma("pool", Wo, V(wv[:, :, o + 1024:o + 1536], self.m_win.bufs))
                P.dma("pool", Wzz, V(wv[:, :, o + 1536:o + 2048], self.m_win.bufs))
                P.dma("pool", Wg, V(wv[:, :, o + 2048:o + 2050], self.m_win.bufs))
                P.dma("sp", nwr, V(self.m_nw.ap[:, h * 512:(h + 1) * 512].partition_broadcast(128), self.m_nw.bufs))
                P.op("pool", lambda e: e.memset(C.ap, 0.0), w=[C])
                P.op("pool", lambda e: e.memset(Cb.ap, 0.0), w=[Cb])
                P.op("pool", lambda e: e.memset(nst.ap, 0.0), w=[nst])
                P.op("pool", lambda e: e.memset(nb.ap, 0.0), w=[nb])
                wl = [Wqk[:, :, j * 128:(j + 1) * 128] for j in range(4)]
                cidx = [2 * h, 2 * h + 1, 8 + 2 * h, 8 + 2 * h + 1]
                for sc in range(self.NSC):
                    hs = hTs[sc % 2]
                    QK = QK_[sc % 2]
                    Vv = Vv_[sc % 2]
                    SO = SO_[sc % 2]
                    SZ = SZ_[sc % 2]
                    IG = IG_[sc % 2]
                    Aa = Aa_[sc % 2]
                    P.dma("sp", hs, V(self.hT_d[sc], [self.hT_bufs[sc]]))
                    self.conv_fm(hs, wl, U, QK, cw, cb, cidx, sc == 0)
                    for c in range(4):
                        tok = slice(c * 128, (c + 1) * 128)
                        for wi_, (W_, dst, fn) in enumerate(((Wv, Vv, None), (Wo, SO, AF.Sigmoid), (Wzz, SZ, AF.Silu))):
                            ptm = ptms[(c * 3 + wi_) % 2]
                            for k in range(16):
                                self.mm(ptm, hs[:, k, tok], W_[:, k, :], start=(k == 0), stop=(k == 15))
                            if fn is None:
                                self.cp("act", dst[:, c, :], ptm)
                            else:
                                self.act(dst[:, c, :], ptm, fn)
                        for k in range(16):
                            self.mm(pmisc[:, 0:2], hs[:, k, tok], Wg[:, k, :], start=(k == 0), stop=(k == 15))
                        self.ts("dve", IG[:, c:c + 1], pmisc[:, 0:1], ibr[:, h:h + 1], None, ALU.add)
                        self.act(Aa[:, c:c + 1], pmisc[:, 1:2], AF.Exp, bias=nfb[:, h:h + 1], scale=-1.0)
                        self.act(Aa[:, c:c + 1], Aa[:, c:c + 1], AF.Ln, bias=1.0)
                        self.ts("dve", Aa[:, c:c + 1], Aa[:, c:c + 1], -1.0, None, ALU.mult)
                    for c in range(4):
                        i = sc * 4 + c
                        tok = slice(c * 128, (c + 1) * 128)
                        a = Aa[:, c:c + 1]
                        KT = KT_[i % 2]
                        sm = sm_[i % 2]
                        arhs = arhs_[i % 2]
                        bcum = bcum_[i % 2]
                        ebc = ebc_[i % 2]
                        etot = etot_[i % 2]
                        w2 = w2_[i % 2]
                        w2b = w2b_[i % 2]
                        segs = segs_[i % 2]
                        PT = PT_[i % 2]
                        VW = VW_[i % 2]
                        den = den_[i % 2]
                        rinv = rinv_[i % 2]
                        c2 = c2_[i % 2]
                        t1 = t1_[i % 2]
                        t2 = t2_[i % 2]
                        junk = junk_[i % 2]
                        ssq = ssq_[i % 2]
                        GN = GN_[i % 2]
                        ig = IG[:, c:c + 1]
                        for j in range(2):
                            self.tr(self.pT[:, j * 128:(j + 1) * 128], QK[:, 2 + j, tok])
                        self.cp("act", KT, self.pT[:, 0:256])
                        for j in range(2):
                            self.mm(pmisc[:, 128:256], QK[:, 2 + j, tok], QK[:, j, tok], start=(j == 0), stop=(j == 1))
                        self.tt("dve", sm, pmisc[:, 128:256], self.tri, ALU.mult)
                        self.ts("dve", arhs, self.tri, a, None, ALU.mult)
                        self.mm(pmisc[:, 8:9], self.tri, a)
                        self.mm(pmisc[:, 16:17], self.ones_f, a)
                        self.cp("act", bcum, pmisc[:, 8:9])
                        self.act(ebc, pmisc[:, 8:9], AF.Exp)
                        self.act(etot, pmisc[:, 16:17], AF.Exp)
                        self.tt("dve", w2, pmisc[:, 16:17], bcum, ALU.subtract)
                        self.act(w2, w2, AF.Exp, bias=ig)
                        self.cp("dve", w2b, w2)
                        self.mm(pseg, self.ones_f, arhs, start=True, stop=False)
                        self.mm(pseg, self.identb, self.negm4[:, 0, :], start=False, stop=True)
                        self.ts("dve", segs, pseg, bcum, None, ALU.subtract)
                        self.act(segs, segs, AF.Exp, bias=ig)
                        self.tt("pool", PT, segs, sm, ALU.mult)
                        self.ts("dve", VW, Vv[:, c, :], w2, None, ALU.mult)
                        self.mm(pyd, PT, Vv[:, c, :])
                        for j in range(2):
                            self.mm(pyo, QK[:, j, tok], Cb[:, j, :], start=(j == 0), stop=(j == 1))
                        self.mm(pmisc[:, 24:25], PT, self.ones_b)
                        for j in range(2):
                            self.mm(pmisc[:, 25:26], QK[:, j, tok], nb[:, j:j + 1], start=(j == 0), stop=(j == 1))
                        for j in range(2):
                            self.mm(pdc[j], KT[:, j * 128:(j + 1) * 128], VW)
                            self.mm(pmisc[:, 32 + j:33 + j], KT[:, j * 128:(j + 1) * 128], w2b)
                        self.tt("dve", den, pmisc[:, 25:26], ebc, ALU.mult)
                        self.tt("dve", den, pmisc[:, 24:25], den, ALU.add)
                        self.ts("dve", c2, den, -1.0, None, ALU.mult)
                        self.tt("dve", den, den, c2, ALU.max)
                        self.ts("dve", den, den, 1.0, None, ALU.max)
                        self.P.op("dve", lambda e, o_=rinv.ap, i_=den.ap: e.reciprocal(o_, i_), r=[den], w=[rinv])
                        self.tt("dve", c2, ebc, rinv, ALU.mult)
                        self.ts("dve", t2, pyo, c2, None, ALU.mult)
                        self.stt("dve", t1, pyd, rinv, t2, ALU.mult, ALU.add)
                        for j in range(2):
                            self.stt("dve", C[:, j, :], C[:, j, :], etot, pdc[j], ALU.mult, ALU.add)
                            self.cp("act", Cb[:, j, :], C[:, j, :])
                        self.stt("dve", nst, nst, etot, pmisc[:, 32:34], ALU.mult, ALU.add)
                        self.cp("dve", nb, nst)
                        self.tt("pool", t1, t1, SO[:, c, :], ALU.mult)
                        self.act(junk, t1, AF.Square, accum=ssq)
                        self.rstd(ssq, 512)
                        self.stt("dve", t1, t1, ssq, nwr, ALU.mult, ALU.mult)
                        self.tt("pool", GN, t1, SZ[:, c, :], ALU.mult)
                        self.store_yT(GN, i, h)

    def phaseC(self, wout, resf, l, mod, next_mod=False, tail=None):
        P = self.P
        R = self.NST * 128
        with P.scope():
            if next_mod:
                mod2 = P.sb("mod2", [128, 5120], F32)
                self.phase0_body(l, mod2, 4)
            wvv = wout.ap.rearrange("(c p) n -> p c n", p=128)
            Whs = [P.sb("Wh%d" % q, [128, 8, 1024], BF16) for q in range(4)]
            ys = [P.sb("ys%d" % i, [128, 32, 128], BF16) for i in range(2 if next_mod else 3)]
            xt = [P.sb("xtC%d" % i, [128, 1024], F32) for i in range(2)]
            xo = [P.sb("xoC%d" % i, [128, 1024], F32) for i in range(2)]
            for q in range(4):
                P.dma("pool", Whs[q], V(wvv[:, q * 8:(q + 1) * 8, :], wout.bufs))
            for i in range(self.NST):
                y_ = ys[i % len(ys)]
                pt, off = i // self.NY, (i % self.NY) * 128
                for rk in range(2):
                    src = self.yT_all[:, pt, rk * self.RY + off: rk * self.RY + off + 128, :].rearrange("g p n -> p g n")
                    P.dma("sp", y_[:, rk * 16:(rk + 1) * 16, :].re("p (g q) t -> p g (q t)", g=4),
                          V(src, [self.yT_all_bufs[g_][pt] for g_ in range(4)]), nbytes=128 * 4 * 512 * 2)
                x_ = xt[i % 2]
                P.dma("sp", x_, resf(i))
                o_ = xo[i % 2]
                for n2 in range(2):
                    ps = self.pb[(i * 2 + n2) % 4]
                    for c in range(32):
                        self.mm(ps, y_[:, c, :], Whs[c // 8][:, c % 8, n2 * 512:(n2 + 1) * 512], start=(c == 0), stop=(c == 31))
                    col = slice(2 * D + n2 * 512, 2 * D + (n2 + 1) * 512)
                    self.tt("dve", o_[:, n2 * 512:(n2 + 1) * 512], ps, mod[:, col], ALU.mult)
                    self.tt("pool", o_[:, n2 * 512:(n2 + 1) * 512], o_[:, n2 * 512:(n2 + 1) * 512],
                            x_[:, n2 * 512:(n2 + 1) * 512], ALU.add)
                q, off = i // self.NQ, (i % self.NQ) * 128
                self.store(V(self.xl[l][q, off:off + 128, :], [self.xl_bufs[l][q]]), o_)
                if i % self.NQ == self.NQ - 1:
                    self.allgather(self.xl[l][q], self.xl_bufs[l][q], self.xa[l][q], self.xa_bufs[l][q],
                                   2 * self.NQ * 128 * 1024 * 4)
            if next_mod:
                self.cp("dve", mod, mod2)
            if tail is not None:
                tail()

    def res_xh(self, i):
        return V(self.xh.ap[i * 128:(i + 1) * 128, :], self.xh.bufs)

    def res_loc(self, l):
        def f(i):
            q, off = i // self.NQ, (i % self.NQ) * 128
            return V(self.xl[l][q, off:off + 128, :], [self.xl_bufs[l][q]])
        return f

    def phaseD(self, srcf):
        with self.P.scope():
            self.phaseD_body(srcf)

    def phaseD_body(self, srcf):
        P = self.P
        if True:
            fw_ = self.rowload("fnw", self.fnorm_w.ap, D, self.fnorm_w)
            xt = [P.sb("xtD%d" % i, [128, D], F32) for i in range(2)]
            ot = [P.sb("otD%d" % i, [128, D], F32) for i in range(2)]
            junk = P.sb("junkD", [128, D], BF16)
            ss = [P.sb("ssD%d" % i, [128, 1], F32) for i in range(2)]
            for i in range(self.NST):
                x_ = xt[i % 2]
                P.dma("sp", x_.re("p (r n) -> p r n", r=2), srcf(i))
                s_ = ss[i % 2]
                self.act(junk, x_, AF.Square, accum=s_)
                self.rstd(s_, D)
                self.stt("dve", ot[i % 2], x_, s_, fw_, ALU.mult, ALU.mult)
                self.store(V(self.out[i * 128:(i + 1) * 128, :], [self.out_bufs[i]]), ot[i % 2])

    def build(self, upto="all"):
        P = self.P
        mod = P.sb("mod", [128, 5120], F32)
        self.phaseA(self.src_x, mod, phase0_l=0)
        self.phaseB_ssd()
        self.phaseC(self.s_wout, self.res_xh, 1, mod, next_mod=(upto != "l1"))
        if upto == "l1":
            self.phaseD(self.src_all(1))
        else:
            self.phaseA(self.src_all(1), mod)
            self.phaseB_ml()
            self.phaseC(self.m_wout, self.res_loc(1), 2, mod, tail=lambda: self.phaseD_body(self.src_all(2)))
        P.op("sp", lambda e: e.nop(), r=[V(self.out, self.out_bufs)])
        P.emit(sched=self.sched)
        print("sim_time_ms", getattr(P, "sim_time", 0) / 1e6, {e: round(b / 1e6, 2) for e, b in getattr(P, "sim_busy", {}).items()})
        return self.nc


def _shared_inputs(inp, r):
    f = np.float32
    ii = np.eye(128, dtype=f)
    tri = np.triu(np.ones((128, 128), f))
    negm = np.where(np.arange(128)[None, :] < np.arange(128)[:, None], f(NEG), f(0)).astype(f)
    consts = np.stack([ii, tri, negm, np.ones((128, 128), f)], axis=1)
    ar = np.arange
    scols, sch = [], []
    for gl in range(4):
        g = 4 * r + gl
        scols += [g * 512 + ar(512), 4096 + g * 512 + ar(512), 8192 + g * 128 + ar(128), 9216 + g * 128 + ar(128),
                  10240 + g * 8 + ar(8)]
    scols = np.concatenate(scols)
    sch = [4 * (4 * r + gl) + j for gl in range(4) for j in range(4)] + [32 + 4 * r + gl for gl in range(4)] + \
          [40 + 4 * r + gl for gl in range(4)]
    mcols = []
    for hl in range(4):
        h = 4 * r + hl
        mcols += [h * 256 + ar(256), 2048 + h * 256 + ar(256), 4096 + h * 512 + ar(512), 8192 + h * 512 + ar(512),
                  12288 + h * 512 + ar(512), np.array([16384 + h, 16392 + h])]
    mcols = np.concatenate(mcols)
    mch = [2 * (4 * r + hl) + j for hl in range(4) for j in range(2)] + [16 + 2 * (4 * r + hl) + j for hl in range(4) for j in range(2)]
    acols = np.concatenate([ar(4096), 4096 + r * 1024 + ar(1024)])
    cw = lambda w: np.ascontiguousarray(w.T.reshape(-1, 128, 4).transpose(1, 0, 2))
    cbv = lambda v: np.ascontiguousarray(v.reshape(-1, 128).T)
    osl = slice(r * 1024, (r + 1) * 1024)
    hs32 = slice(32 * r, 32 * r + 32)
    return {
        "norm_w": inp["norm_w"], "final_norm_w": inp["final_norm_w"].reshape(1, -1),
        "ada_w": np.ascontiguousarray(inp["ada_w"][:, :, acols]), "ada_b": np.ascontiguousarray(inp["ada_b"][:, acols]),
        "ssd_w_in": np.ascontiguousarray(inp["ssd_w_in"][0][:, scols]),
        "ssd_w_out": np.ascontiguousarray(inp["ssd_w_out"][0][:, osl]),
        "ssd_cw": np.ascontiguousarray(cw(inp["ssd_conv_w"][0])[:, sch, :]),
        "ssd_cb": np.ascontiguousarray(cbv(inp["ssd_conv_b"][0])[:, sch]),
        "ssd_dt_bias": np.ascontiguousarray(inp["ssd_dt_bias"][:, hs32]),
        "ssd_a_log": np.ascontiguousarray(inp["ssd_a_log"][:, hs32]),
        "ssd_d": np.ascontiguousarray(inp["ssd_d"][:, hs32]),
        "ssd_norm_w": np.ascontiguousarray(inp["ssd_norm_w"][:, r * 2048:(r + 1) * 2048]),
        "ml_w_in": np.ascontiguousarray(inp["ml_w_in"][0][:, mcols]),
        "ml_w_out": np.ascontiguousarray(inp["ml_w_out"][0][:, osl]),
        "ml_cw": np.ascontiguousarray(cw(inp["ml_conv_w"][0])[:, mch, :]),
        "ml_cb": np.ascontiguousarray(cbv(inp["ml_conv_b"][0])[:, mch]),
        "ml_igate_b": np.ascontiguousarray(inp["ml_igate_b"][:, 4 * r:4 * r + 4]),
        "ml_fgate_b": np.ascontiguousarray(inp["ml_fgate_b"][:, 4 * r:4 * r + 4]),
        "ml_norm_w": np.ascontiguousarray(inp["ml_norm_w"][:, r * 2048:(r + 1) * 2048]),
        "consts": consts,
    }


def host_inputs(inp, b, r, shared):
    d = dict(shared[r])
    d["x"] = np.ascontiguousarray(inp["x"][b])
    d["x_half"] = np.ascontiguousarray(inp["x"][b][:, r * 1024:(r + 1) * 1024])
    d["cT"] = np.ascontiguousarray(inp["c"][b].reshape(128, 16))
    return d


_NC_CACHE = {}


def run(inp, upto="all", ncores=8, cut=None):
    inp = {k: np.asarray(v, dtype=np.float32) for k, v in inp.items()}
    B, T, _ = inp["x"].shape
    key = (T, upto, cut)
    if key not in _NC_CACHE:
        kb = KB(T)
        kb.cut = cut
        _NC_CACHE[key] = kb.build(upto)
    nc = _NC_CACHE[key]
    shared = [_shared_inputs(inp, r) for r in range(2)]
    in_maps = [host_inputs(inp, (i // 2) % B, i % 2, shared) for i in range(ncores)]
    res = run_bass_kernel_spmd(nc, in_maps, core_ids=list(range(ncores)))
    return np.stack([res.results[2 * b]["out"] for b in range(B)], axis=0)


def kernel(**inputs):
    return run(inputs, "all", 8).astype(np.float32)
```

```python
from contextlib import ExitStack
import numpy as np
import concourse.bass as bass
import concourse.mybir as mybir
from concourse.bass_utils import run_bass_kernel_spmd

F32 = mybir.dt.float32
BF16 = mybir.dt.bfloat16
AF = mybir.ActivationFunctionType
ALU = mybir.AluOpType
AX = mybir.AxisListType


class Buf:
    __slots__ = ("name", "writers", "readers", "sem", "dcnt", "excl")

    def __init__(self, name, excl=False):
        self.name = name
        self.excl = excl
        self.writers = []
        self.readers = []
        self.sem = None
        self.dcnt = 0


class V:
    __slots__ = ("ap", "bufs")

    def __init__(self, ap, bufs):
        self.ap = ap
        self.bufs = tuple(bufs)

    def __getitem__(self, k):
        return V(self.ap[k], self.bufs)

    def re(self, s, **kw):
        return V(self.ap.rearrange(s, **kw), self.bufs)

    def bc(self, shape):
        return V(self.ap.to_broadcast(shape), self.bufs)


class Op:
    __slots__ = ("eng", "fn", "deps", "odeps", "signal", "val", "sem", "is_dma", "name", "inc", "cost", "lat", "epoch", "seq", "fin", "npred", "succ", "rt", "st", "crit", "line")


class Prog:
    ENGS = ["pe", "act", "dve", "pool", "sp"]
    BLK = {"pe": "tensor", "act": "scalar", "dve": "vector", "pool": "gpsimd", "sp": "sync"}

    def __init__(self, nc):
        self.nc = nc
        self.ops = {e: [] for e in self.ENGS}
        self.stack = ExitStack()
        self.dma_bufs = []
        self.nsb = 0
        self.all_dma_out = []
        self.pending_dma = []
        self.stacks = [self.stack]
        self.bar_scr = None
        self.epoch = 0
        self.nseq = 0
        self.debug_lines = False

    def sb(self, name, shape, dtype):
        self.nsb += 1
        t = self.stacks[-1].enter_context(self.nc.sbuf_tensor("%s_%d" % (name, self.nsb), list(shape), dtype))
        return V(t.ap(), [Buf(name)])

    def ps(self, name, shape, dtype=F32):
        t = self.stack.enter_context(self.nc.psum_tensor(name, list(shape), dtype))
        return V(t.ap(), [Buf(name, excl=True)])

    def dram(self, name, shape, dtype, kind="Internal"):
        t = self.nc.dram_tensor(name, list(shape), dtype, kind=kind)
        return V(t.ap(), [Buf(name)])

    def op(self, eng, fn, r=(), w=(), dma=False, key=None, name=None, extra=(), cost=100.0, lat=0.0, inc=16):
        op = Op()
        op.inc = inc
        op.eng, op.fn, op.signal, op.val, op.sem, op.is_dma, op.name = eng, fn, False, 0, None, dma, name
        op.deps = []
        op.odeps = []
        op.cost, op.lat, op.epoch = cost, lat, self.epoch
        self.nseq += 1
        op.seq = self.nseq
        op.crit = None
        if self.debug_lines:
            import sys as _s
            f = _s._getframe(1)
            while f.f_code.co_name in ("op", "dma", "mm", "tt", "ts", "stt", "act", "cp", "tr", "store", "rstd", "rowload", "allgather"):
                f = f.f_back
            op.line = f.f_lineno
        else:
            op.line = 0
        rb = [b for v in r for b in v.bufs]
        wb = [b for v in w for b in v.bufs]
        cand = []
        for b in rb:
            cand += b.writers
            if b.excl:
                cand += [x for x in b.readers if x.eng != eng]
        for b in wb:
            cand += b.writers
            cand += b.readers
        cand += list(extra)
        seen = set()
        for d in cand:
            if d is op or id(d) in seen:
                continue
            seen.add(id(d))
            if (not d.is_dma) and (not dma) and d.eng == eng and eng == "pe":
                op.odeps.append(d)
                continue
            d.signal = True
            op.deps.append(d)
        for b in wb:
            if b.readers:
                b.writers = [op]
                b.readers = []
            else:
                b.writers = [x for x in b.writers if not (x.eng == eng and x.is_dma == dma and not dma)] + [op]
        for b in rb:
            b.readers.append(op)
        if dma:
            kb = key if key is not None else (wb[0] if wb else rb[0])
            if kb.sem is None:
                self.dma_bufs.append(kb)
                kb.sem = "pending"
            kb.dcnt += 1
            op.sem = kb
            op.val = inc * kb.dcnt
            op.signal = True
            if inc == 16:
                self.pending_dma.append(op)
        self.ops[eng].append(op)
        return op

    def barrier(self):
        if self.bar_scr is None:
            self.bar_scr = {e: self.sb("bar_" + e, [128, 2], F32) for e in ("act", "dve", "pool")}
            for e in ("dve", "pool"):
                self.op(e, lambda g, a=self.bar_scr[e].ap: g.memset(a, 0.0), w=[self.bar_scr[e]])
            self.op("act", lambda g, a=self.bar_scr["act"].ap, b=self.bar_scr["dve"].ap: g.copy(a, b),
                    r=[self.bar_scr["dve"]], w=[self.bar_scr["act"]])
        self.epoch += 1
        arr = []
        for e in ("dve", "pool"):
            arr.append(self.op(e, lambda g, a=self.bar_scr[e].ap: g.memset(a[:, 0:1], 0.0), w=[self.bar_scr[e]]))
        arr.append(self.op("act", lambda g, a=self.bar_scr["act"].ap: g.copy(a[:, 0:1], a[:, 1:2]),
                           w=[self.bar_scr["act"]]))
        pend = list(self.pending_dma)
        self.pending_dma = []
        for e in self.ENGS:
            self.op(e, lambda g: g.nop(), extra=arr + pend, name="bar")
        self.epoch += 1

    def scope(self):
        return _Scope(self)

    def dma(self, eng, out, in_, key=None, nbytes=None, **kw):
        if nbytes is None:
            n = 1
            for d in out.ap.shape:
                n *= d
            nbytes = n * 4
        o = self.op(eng, lambda e: e.dma_start(out=out.ap, in_=in_.ap, **kw), r=[in_], w=[out], dma=True,
                    key=key, cost=(60.0 if eng == "sp" else 300.0 + nbytes / 2000.0), lat=2000.0 + nbytes / 100.0)
        return o

    def schedule(self):
        import heapq
        allops = [o for e in self.ENGS for o in self.ops[e]]
        allops.sort(key=lambda o: o.seq)
        for o in allops:
            o.succ = []
            o.npred = 0
            o.rt = 0.0
            o.fin = 0.0
        for o in allops:
            for d in o.deps + o.odeps:
                d.succ.append(o)
                o.npred += 1
        prio = {}
        for o in reversed(allops):
            m = 0.0
            for sx in o.succ:
                if sx.epoch == o.epoch:
                    p = prio[id(sx)]
                    if p > m:
                        m = p
            prio[id(o)] = o.cost + o.lat + m
        byep = {}
        for o in allops:
            byep.setdefault(o.epoch, []).append(o)
        tfree = {e: 0.0 for e in self.ENGS}
        busy = {e: 0.0 for e in self.ENGS}
        new = {e: [] for e in self.ENGS}
        tglob = 0.0
        for ep in sorted(byep):
            ops = byep[ep]
            wait = {e: [] for e in self.ENGS}
            ready = {e: [] for e in self.ENGS}
            left = len(ops)
            for o in ops:
                if o.npred == 0:
                    heapq.heappush(wait[o.eng], (o.rt, o.seq, o))
            while left:
                best = None
                for e in self.ENGS:
                    w, r = wait[e], ready[e]
                    while w and w[0][0] <= tfree[e]:
                        _, sq, o2 = heapq.heappop(w)
                        heapq.heappush(r, (-prio[id(o2)], sq, o2))
                    if r:
                        st = tfree[e]
                    elif w:
                        st = w[0][0]
                    else:
                        continue
                    if best is None or st < best[0]:
                        best = (st, e)
                assert best is not None, "scheduler deadlock (cross-epoch dependency?)"
                st, e = best
                if ready[e]:
                    _, _, o = heapq.heappop(ready[e])
                else:
                    _, _, o = heapq.heappop(wait[e])
                tfree[e] = st + o.cost
                o.st = st
                busy[e] += o.cost
                o.fin = st + o.cost + o.lat
                new[e].append(o)
                left -= 1
                for sx in o.succ:
                    lat = 60.0 if (sx.eng == o.eng and not o.is_dma) else 250.0
                    if sx.eng == "pe" and o.eng == "pe":
                        lat = 0.0
                    if o.fin + lat > sx.rt:
                        sx.rt = o.fin + lat
                        sx.crit = o
                    sx.npred -= 1
                    if sx.npred == 0 and sx.epoch == ep:
                        heapq.heappush(wait[sx.eng], (sx.rt, sx.seq, sx))
            t_prev = tglob
            tmax = max([tfree[e] for e in self.ENGS] + [o.fin for o in ops])
            tglob = tmax
            if len(ops) > 50:
                eb = {}
                for o in ops:
                    eb[o.eng] = eb.get(o.eng, 0.0) + o.cost
                print("  epoch %d: %.3f ms  n=%d  busy=%s" % (ep, (tmax - t_prev) / 1e6, len(ops),
                      {e: round(v / 1e6, 2) for e, v in eb.items()}))
            for e in self.ENGS:
                tfree[e] = tmax
        self.ops = new
        self.sim_time = max(tfree.values())
        self.sim_busy = busy

    def emit(self, final_wait_ops=(), sched=True):
        nc = self.nc
        st = self.stack
        if sched:
            self.schedule()
        esem = {e: st.enter_context(nc.semaphore("s_" + e)) for e in self.ENGS}
        for i, b in enumerate(self.dma_bufs):
            b.sem = st.enter_context(nc.semaphore("d%d" % i))
        for e in self.ENGS:
            c = 0
            for op in self.ops[e]:
                if op.is_dma:
                    op.sem = op.sem.sem if isinstance(op.sem, Buf) else op.sem
                elif op.signal:
                    c += 1
                    op.val = c
                    op.sem = esem[e]
        self.counts = {e: sum(1 for o in self.ops[e]) for e in self.ENGS}
        block = st.enter_context(nc.Block())
        for e in self.ENGS:
            ops = self.ops[e]
            if not ops:
                continue

            def body(eng, ops=ops):
                waited = {}
                for op in ops:
                    need = {}
                    for d in op.deps:
                        k = id(d.sem)
                        if k not in need or need[k][1] < d.val:
                            need[k] = (d.sem, d.val)
                    for k, (s, v) in need.items():
                        if waited.get(k, 0) < v:
                            eng.wait_ge(s, v)
                            waited[k] = v
                    ins = op.fn(eng)
                    if op.signal:
                        ins.then_inc(op.sem, op.inc if op.is_dma else 1)

            getattr(block, self.BLK[e])(body)

    def close(self):
        self.stack.close()


class _Scope:
    def __init__(self, P):
        self.P = P

    def __enter__(self):
        self.st = ExitStack()
        self.P.stacks.append(self.st)
        return self

    def __exit__(self, *a):
        self.P.barrier()
        self.P.stacks.pop()
        self.st.close()
        return False


def gap_report(P, epoch, eng="pe", top=12):
    ops = [o for o in P.ops[eng] if o.epoch == epoch]
    agg = {}
    prev_end = ops[0].st
    for o in ops:
        gap = o.st - prev_end
        if gap > 1.0:
            c = o.crit
            key = (o.line, c.eng if c else None, c.line if c else None, c.is_dma if c else None)
            a = agg.setdefault(key, [0.0, 0])
            a[0] += gap
            a[1] += 1
        prev_end = o.st + o.cost
    tot = sum(a[0] for a in agg.values())
    print("gap report epoch %d eng %s: total gap %.3f ms" % (epoch, eng, tot / 1e6))
    for key, a in sorted(agg.items(), key=lambda kv: -kv[1][0])[:top]:
        print("   op@%s waits on %s@%s dma=%s : %.3f ms (%d)" % (key[0], key[1], key[2], key[3], a[0] / 1e6, a[1]))


D = 2048
KC = 16
EPS = 1e-6
NEG = -30000.0


class Cut(Exception):
    pass


class KB:
    cut = None
    sched = True

    def ck(self, n):
        if self.cut is not None and self.cut == n:
            raise Cut()

    def __init__(self, T, nlayers=2):
        self.T = T
        self.NSC = T // 512
        self.NST = T // 128
        nc = bass.Bass("TRN2", target_bir_lowering=False)
        self.nc = nc
        P = Prog(nc)
        self.P = P
        ei = lambda n, s, d=F32: P.dram(n, s, d, kind="ExternalInput")
        self.x = ei("x", [T, D])
        self.xh = ei("x_half", [T, 1024])
        self.cT = ei("cT", [128, 16])
        self.norm_w = ei("norm_w", [2, D])
        self.fnorm_w = ei("final_norm_w", [1, D])
        self.ada_w = ei("ada_w", [2, D, 5120])
        self.ada_b = ei("ada_b", [2, 5120])
        self.s_win = ei("ssd_w_in", [D, 5152])
        self.s_wout = ei("ssd_w_out", [4096, 1024])
        self.s_cw = ei("ssd_cw", [128, 24, 4])
        self.s_cb = ei("ssd_cb", [128, 24])
        self.s_dtb = ei("ssd_dt_bias", [1, 32])
        self.s_alog = ei("ssd_a_log", [1, 32])
        self.s_d = ei("ssd_d", [1, 32])
        self.s_nw = ei("ssd_norm_w", [1, 2048])
        self.m_win = ei("ml_w_in", [D, 8200])
        self.m_wout = ei("ml_w_out", [4096, 1024])
        self.m_cw = ei("ml_cw", [128, 16, 4])
        self.m_cb = ei("ml_cb", [128, 16])
        self.m_ib = ei("ml_igate_b", [1, 4])
        self.m_fb = ei("ml_fgate_b", [1, 4])
        self.m_nw = ei("ml_norm_w", [1, 2048])
        self.consts = ei("consts", [128, 4, 128])
        self.out = P.nc.dram_tensor("out", [T, D], F32, kind="ExternalOutput").ap()
        self.out_bufs = [Buf("out%d" % i) for i in range(self.NST)]
        self.hT_d = nc.dram_tensor("hT_d", [self.NSC, 128, 16, 512], BF16).ap()
        self.hT_bufs = [Buf("hT%d" % i) for i in range(self.NSC)]
        R = self.NST * 128
        self.YC = max(1, R // 2048)
        self.RY = R // self.YC
        self.NY = self.NST // self.YC
        self.yT_loc = nc.dram_tensor("yT_loc", [4, self.YC, self.RY, 512], BF16).ap()
        self.yT_loc_bufs = [[Buf("yTl%d_%d" % (i, j)) for j in range(self.YC)] for i in range(4)]
        self.yT_all = nc.dram_tensor("yT_all", [4, self.YC, 2 * self.RY, 512], BF16).ap()
        self.yT_all_bufs = [[Buf("yTa%d_%d" % (i, j)) for j in range(self.YC)] for i in range(4)]
        self.NXC = max(4, T // 512)
        self.NQ = self.NST // self.NXC
        TQ = T // self.NXC
        self.xl, self.xa, self.xl_bufs, self.xa_bufs = {}, {}, {}, {}
        for l in (1, 2):
            self.xl[l] = nc.dram_tensor("x%d_loc" % l, [self.NXC, TQ, 1024], F32).ap()
            self.xa[l] = nc.dram_tensor("x%d_all" % l, [self.NXC, 2 * TQ, 1024], F32).ap()
            self.xl_bufs[l] = [Buf("x%dl%d" % (l, i)) for i in range(self.NXC)]
            self.xa_bufs[l] = [Buf("x%da%d" % (l, i)) for i in range(self.NXC)]
        self.cc_key = Buf("cc")
        self.xin_bufs = [self.x.bufs[0]] * self.NST
        self.pb = [P.ps("pb%d" % i, [128, 512], F32) for i in range(7)]
        self.pT = P.ps("pT", [128, 1024], BF16)
        self.pT2 = V(self.pb[0].ap.bitcast(BF16), self.pb[0].bufs)
        self.setup_consts()
        P.barrier()

    @staticmethod
    def fsz(v):
        n = 1
        for d in v.ap.shape[1:]:
            n *= d
        return n

    def ecost(self, eng, out, *ins):
        F = self.fsz(out)
        if eng == "dve":
            return 90.0 + F / 0.8
        if eng == "act":
            return 220.0 + F / 0.9
        return 120.0 + F / 0.5

    def mm(self, out, lhsT, rhs, start=True, stop=True):
        N = self.fsz(rhs)
        f32 = lhsT.ap.dtype == F32
        c = max(N, 64) / 2.4 * (4 if f32 else 1) + (180.0 if f32 else 90.0) + 10.0
        self.P.op("pe", lambda e: e.matmul(out.ap, lhsT.ap, rhs.ap, start=start, stop=stop), r=[lhsT, rhs], w=[out], cost=c)

    def tr(self, out, in_):
        idb = self.identb
        self.P.op("pe", lambda e: e.transpose(out.ap, in_.ap, idb.ap), r=[in_, idb], w=[out], cost=155.0)

    def tt(self, eng, out, in0, in1, op):
        self.P.op(eng, lambda e: e.tensor_tensor(out.ap, in0.ap, in1.ap, op), r=[in0, in1], w=[out],
                  cost=self.ecost(eng, out))

    def ts(self, eng, out, in0, s1, s2, op0, op1=None):
        r = [in0] + [s for s in (s1, s2) if isinstance(s, V)]
        a1 = s1.ap if isinstance(s1, V) else s1
        a2 = s2.ap if isinstance(s2, V) else s2
        c = self.ecost(eng, out)
        if op1 is None:
            self.P.op(eng, lambda e: e.tensor_single_scalar(out.ap, in0.ap, a1, op0), r=r, w=[out], cost=c)
        else:
            self.P.op(eng, lambda e: e.tensor_scalar(out.ap, in0.ap, a1, a2, op0, op1), r=r, w=[out], cost=c)

    def stt(self, eng, out, in0, sc, in1, op0, op1):
        r = [in0, in1] + ([sc] if isinstance(sc, V) else [])
        a = sc.ap if isinstance(sc, V) else sc
        self.P.op(eng, lambda e: e.scalar_tensor_tensor(out.ap, in0.ap, a, in1.ap, op0, op1), r=r, w=[out],
                  cost=self.ecost(eng, out))

    def act(self, out, in_, func, bias=None, scale=None, accum=None):
        r = [in_] + [s for s in (bias, scale) if isinstance(s, V)]
        w = [out] + ([accum] if accum is not None else [])
        kw = {}
        if bias is not None:
            kw["bias"] = bias.ap if isinstance(bias, V) else bias
        if scale is not None:
            kw["scale"] = scale.ap if isinstance(scale, V) else scale
        if accum is not None:
            kw["accum_out"] = accum.ap
        self.P.op("act", lambda e: e.activation(out.ap, in_.ap, func, **kw), r=r, w=w, cost=self.ecost("act", out))

    def cp(self, eng, out, in_):
        c = self.ecost(eng, out)
        if eng == "act":
            self.P.op("act", lambda e: e.copy(out.ap, in_.ap), r=[in_], w=[out], cost=c)
        else:
            self.P.op(eng, lambda e: e.tensor_copy(out.ap, in_.ap), r=[in_], w=[out], cost=c)

    def allgather(self, src_ap, src_buf, dst_ap, dst_buf, nbytes):
        rg = [[0, 1], [2, 3], [4, 5], [6, 7]]
        self.P.op("pool", lambda e: e.collective_compute("AllGather", ALU.bypass, replica_groups=rg,
                                                         ins=[src_ap.opt()], outs=[dst_ap.opt()]),
                  r=[V(src_ap, [src_buf])], w=[V(dst_ap, [dst_buf])], dma=True, inc=1, key=self.cc_key,
                  cost=2000.0, lat=30000.0 + nbytes / 60.0)

    def src_x(self, i):
        return V(self.x.ap[i * 128:(i + 1) * 128, :].rearrange("p (r n) -> p r n", r=2), self.x.bufs)

    def src_all(self, l):
        def f(i):
            q, o = i // self.NQ, (i % self.NQ) * 128
            ap = self.xa[l][q].rearrange("(r t) n -> t r n", r=2)[o:o + 128]
            return V(ap, [self.xa_bufs[l][q]])
        return f

    def store(self, dst, src):
        self.P.dma("sp", dst, src, key=src.bufs[0])

    def rowload(self, name, src_ap, n, src_v):
        t = self.P.sb(name, [128, n], F32)
        self.P.dma("sp", t, V(src_ap.partition_broadcast(128), src_v.bufs))
        return t

    def rstd(self, ss, n):
        self.ts("dve", ss, ss, 1.0 / n, EPS, ALU.mult, ALU.add)
        self.act(ss, ss, AF.Sqrt)
        self.P.op("dve", lambda e: e.reciprocal(ss.ap, ss.ap), r=[ss], w=[ss])

    def setup_consts(self):
        P = self.P
        cf = P.sb("cf", [128, 4, 128], F32)
        P.dma("sp", cf, self.consts)
        self.ident_f = cf[:, 0, :]
        self.tri = cf[:, 1, :]
        self.ones_f = cf[:, 3, :]
        self.identb = P.sb("identb", [128, 128], BF16)
        self.cp("dve", self.identb, cf[:, 0, :])
        self.negm4 = P.sb("negm4", [128, 4, 128], BF16)
        self.cp("dve", self.negm4, V(cf.ap[:, 2, :].unsqueeze(1).to_broadcast([128, 4, 128]), cf.bufs))
        self.ones_b = P.sb("ones_b", [128, 1], BF16)
        self.cp("dve", self.ones_b, cf[:, 3, 0:1])

    def phase0(self, l, mod):
        with self.P.scope():
            self.phase0_body(l, mod, 0)

    def phase0_body(self, l, mod, pbase):
        P = self.P
        if True:
            cs = P.sb("cs", [128, 16], F32)
            P.dma("sp", cs, self.cT)
            self.act(cs, cs, AF.Silu)
            condB = P.sb("condB", [128, 16, 128], BF16)
            self.cp("dve", condB, V(cs.ap.unsqueeze(2).to_broadcast([128, 16, 128]), cs.bufs))
            adab = self.rowload("adab", self.ada_b.ap[l:l + 1, :], 5120, self.ada_b)
            nw = self.rowload("nw", self.norm_w.ap[l:l + 1, :], D, self.norm_w)
            wv = self.ada_w.ap[l].rearrange("(p k) n -> p k n", k=16)
            slabs = [P.sb("adaw%d" % i, [128, 16, 512], BF16) for i in range(2)]
            for n in range(10):
                sl = slabs[n % 2]
                P.dma("pool", sl, V(wv[:, :, n * 512:(n + 1) * 512], self.ada_w.bufs))
                ps = self.pb[pbase + n % 2]
                for k in range(16):
                    self.mm(ps, condB[:, k, :], sl[:, k, :], start=(k == 0), stop=(k == 15))
                self.tt("dve", mod[:, n * 512:(n + 1) * 512], ps, adab[:, n * 512:(n + 1) * 512], ALU.add)
            self.stt("dve", mod[:, D:2 * D], mod[:, D:2 * D], 1.0, nw, ALU.add, ALU.mult)

    def phaseA(self, srcf, mod, phase0_l=None):
        P = self.P
        with P.scope():
            if phase0_l is not None:
                self.phase0_body(phase0_l, mod, 0)
            xt = [P.sb("xt%d" % i, [128, D], F32) for i in range(3)]
            hn_ = [P.sb("hn%d" % i, [128, D], F32) for i in range(2)]
            hb_ = [P.sb("hb%d" % i, [128, D], BF16) for i in range(2)]
            junk_ = [P.sb("junkA%d" % i, [128, D], BF16) for i in range(2)]
            ss = [P.sb("ssA%d" % i, [128, 1], F32) for i in range(2)]
            stage = [P.sb("hTst%d" % i, [128, 16, 512], BF16) for i in range(2)]
            for i in range(self.NST):
                sc, c = i // 4, i % 4
                x_ = xt[i % 3]
                hn, hb, junk = hn_[i % 2], hb_[i % 2], junk_[i % 2]
                P.dma("sp", x_.re("p (r n) -> p r n", r=2), srcf(i))
                s_ = ss[i % 2]
                self.act(junk, x_, AF.Square, accum=s_)
                self.rstd(s_, D)
                self.stt("dve", hn, x_, s_, mod[:, D:2 * D], ALU.mult, ALU.mult)
                self.tt("pool", hb, hn, mod[:, 0:D], ALU.add)
                st = stage[sc % 2]
                for half in range(2):
                    for kk in range(8):
                        k = half * 8 + kk
                        self.tr(self.pT[:, kk * 128:(kk + 1) * 128], hb[:, k * 128:(k + 1) * 128])
                    self.cp("act" if half == 0 else "dve", st[:, half * 8:(half + 1) * 8, c * 128:(c + 1) * 128],
                            self.pT.re("p (k t) -> p k t", k=8))
                if c == 3:
                    self.store(V(self.hT_d[sc], [self.hT_bufs[sc]]), st)

    def conv_fm(self, hs, wlist, U, XB, cw, cb, cidx, first):
        P = self.P
        nj = len(wlist)
        for j in range(nj):
            ps = self.pb[j % 2]
            for k in range(16):
                self.mm(ps, wlist[j][:, k, :], hs[:, k, :], start=(k == 0), stop=(k == 15))
            Uj = U[j % len(U)]
            Hj = self.Uh[j]
            if first:
                self.P.op("pool", lambda e, a=Uj.ap[:, 0:3]: e.memset(a, 0.0), w=[Uj])
            else:
                self.cp("pool", Uj[:, 0:3], Hj)
            self.cp("act", Uj[:, 3:515], ps)
            acc = self.cacc[j % 2]
            ci = cidx[j]
            self.ts("dve", acc, Uj[:, 0:512], cw[:, ci, 0:1], cb[:, ci:ci + 1], ALU.mult, ALU.add)
            for k in range(1, 4):
                self.stt("dve", acc, Uj[:, k:k + 512], cw[:, ci, k:k + 1], acc, ALU.mult, ALU.add)
            self.act(XB[:, j, :], acc, AF.Silu)
            self.cp("pool", Hj, Uj[:, 512:515])

    def store_yT(self, GN, i, cbase):
        P = self.P
        for q in range(4):
            self.tr(self.pT2[:, q * 128:(q + 1) * 128], GN[:, q * 128:(q + 1) * 128])
        ys = self.yst[i % 2]
        self.cp("act", ys, self.pT2[:, 0:512].re("p (q t) -> p q t", q=4))
        gl = cbase
        pt, off = i // self.NY, (i % self.NY) * 128
        self.store(V(self.yT_loc[gl, pt, off:off + 128, :], [self.yT_loc_bufs[gl][pt]]), ys.re("p q t -> p (q t)"))
        if i % self.NY == self.NY - 1:
            self.allgather(self.yT_loc[gl, pt], self.yT_loc_bufs[gl][pt], self.yT_all[gl, pt], self.yT_all_bufs[gl][pt],
                           2 * self.RY * 512 * 2)

    def phaseB_ssd(self):
        try:
            self.phaseB_ssd_()
        except Cut:
            self.P.stacks.pop()
            self.P.barrier()

    def phaseB_ssd_(self):
        P = self.P
        T = self.T
        with P.scope():
            wv = self.s_win.ap.rearrange("(k p) n -> p k n", p=128)
            Wz = [P.sb("Wz%d" % i, [128, 16, 512], BF16) for i in range(2)]
            Wx = [P.sb("Wx%d" % i, [128, 16, 512], BF16) for i in range(2)]
            Wbc = [P.sb("Wbc%d" % i, [128, 16, 256], BF16) for i in range(2)]
            Wdt = P.sb("WdtA", [128, 16, 32], BF16)
            hTs = [P.sb("hTs%d" % i, [128, 16, 512], BF16) for i in range(2)]
            cw = P.sb("cw", [128, 24, 4], F32)
            cb = P.sb("cb", [128, 24], F32)
            P.dma("sp", cw, self.s_cw)
            P.dma("sp", cb, self.s_cb)
            dtb = self.rowload("dtb", self.s_dtb.ap, 32, self.s_dtb)
            aneg = self.rowload("aneg", self.s_alog.ap, 32, self.s_alog)
            self.act(aneg, aneg, AF.Exp)
            self.ts("dve", aneg, aneg, -1.0, None, ALU.mult)
            dsk = self.rowload("dsk", self.s_d.ap, 32, self.s_d)
            nwr = P.sb("nwr", [128, 512], F32)
            U = [P.sb("U%d" % j, [128, 515], F32) for j in range(3)]
            self.Uh = [P.sb("Uh%d" % j, [128, 3], F32) for j in range(6)]
            self.cacc = [P.sb("cacc%d" % j, [128, 512], F32) for j in range(2)]
            XB_ = [P.sb("XB%d" % q_, [128, 6, 512], BF16) for q_ in range(2)]
            SZ_ = [P.sb("SZ%d" % q_, [128, 4, 512], BF16) for q_ in range(2)]
            DT_ = [P.sb("DT%d" % q_, [128, 4, 32], F32) for q_ in range(self.NSC)]
            Aa_ = [P.sb("Aa%d" % q_, [128, 4, 8], F32) for q_ in range(2)]
            XT_ = [P.sb("XT%d" % q_, [128, 512], BF16) for q_ in range(2)]
            BT_ = [P.sb("BT%d" % q_, [128, 128], BF16) for q_ in range(2)]
            cbm_ = [P.sb("cbm%d" % q_, [128, 128], F32) for q_ in range(2)]
            arhs_ = [P.sb("arhs", [128, 8, 128], F32)] * 2
            acs_ = [P.sb("acs%d" % q_, [128, 8], F32) for q_ in range(2)]
            eacs_ = [P.sb("eacs%d" % q_, [128, 8], F32) for q_ in range(2)]
            etot_ = [P.sb("etot%d" % q_, [128, 8], F32) for q_ in range(2)]
            wend_ = [P.sb("wend%d" % q_, [128, 8], F32) for q_ in range(2)]
            segs_ = [P.sb("segs", [128, 4, 128], F32)] * 2
            MT_ = [P.sb("MT%d" % q_, [128, 8, 128], BF16) for q_ in range(2)]
            XDT_ = [P.sb("XDT%d" % q_, [128, 512], BF16) for q_ in range(2)]
            XW_ = [P.sb("XW%d" % q_, [128, 512], BF16) for q_ in range(2)]
            S = P.sb("S", [128, 512], F32)
            Sb = P.sb("Sb", [128, 512], BF16)
            t1_ = [P.sb("t1%d" % q_, [128, 512], F32) for q_ in range(2)]
            t2_ = [P.sb("t2", [128, 512], F32)] * 2
            ssq_ = [P.sb("ssq%d" % q_, [128, 1], F32) for q_ in range(2)]
            GN_ = [P.sb("GN%d" % q_, [128, 512], BF16) for q_ in range(2)]
            self.yst = [P.sb("yst%d" % i, [128, 4, 128], BF16) for i in range(2)]
            pmisc, pseg, pyd, pyo, pds, pz = self.pb[3], self.pb[4], self.pb[5], self.pb[6], self.pb[2], self.pb[2]

            def loadW(g):
                b = g % 2
                o = g * 1288
                P.dma("pool", Wz[b], V(wv[:, :, o:o + 512], self.s_win.bufs))
                P.dma("pool", Wx[b], V(wv[:, :, o + 512:o + 1024], self.s_win.bufs))
                P.dma("pool", Wbc[b], V(wv[:, :, o + 1024:o + 1280], self.s_win.bufs))

            for g_ in range(4):
                P.dma("pool", Wdt[:, :, g_ * 8:(g_ + 1) * 8], V(wv[:, :, g_ * 1288 + 1280:g_ * 1288 + 1288], self.s_win.bufs))
            loadW(0)
            for g in range(4 if self.cut is None else 1):
                b = g % 2
                if g + 1 < 4:
                    loadW(g + 1)
                P.op("pool", lambda e: e.memset(S.ap, 0.0), w=[S])
                P.op("pool", lambda e: e.memset(Sb.ap, 0.0), w=[Sb])
                P.dma("sp", nwr, V(self.s_nw.ap[:, g * 512:(g + 1) * 512].partition_broadcast(128), self.s_nw.bufs))
                wl = [Wx[b][:, :, j * 128:(j + 1) * 128] for j in range(4)] + [Wbc[b][:, :, 0:128], Wbc[b][:, :, 128:256]]
                cidx = [g * 4 + j for j in range(4)] + [16 + g, 20 + g]
                for sc in range(self.NSC):
                    hs = hTs[sc % 2]
                    XB = XB_[sc % 2]
                    SZ = SZ_[sc % 2]
                    DT = DT_[sc]
                    Aa = Aa_[sc % 2]
                    gs = slice(g * 8, (g + 1) * 8)
                    P.dma("sp", hs, V(self.hT_d[sc], [self.hT_bufs[sc]]))
                    self.ck(1)
                    self.conv_fm(hs, wl, U, XB, cw, cb, cidx, sc == 0)
                    self.ck(2)
                    for c in range(4):
                        tok = slice(c * 128, (c + 1) * 128)
                        for k in range(16):
                            self.mm(pz, hs[:, k, tok], Wz[b][:, k, :], start=(k == 0), stop=(k == 15))
                        self.act(SZ[:, c, :], pz, AF.Silu)
                        if g == 0:
                            for k in range(16):
                                self.mm(pmisc[:, 256:288], hs[:, k, tok], Wdt[:, k, :], start=(k == 0), stop=(k == 15))
                            self.tt("dve", DT[:, c, :], pmisc[:, 256:288], dtb, ALU.add)
                            self.act(DT[:, c, :], DT[:, c, :], AF.Exp)
                            self.act(DT[:, c, :], DT[:, c, :], AF.Ln, bias=1.0)
                        self.tt("dve", Aa[:, c, :], DT[:, c, gs], aneg[:, gs], ALU.mult)
                    self.ck(3)
                    for c in range(4):
                        i = sc * 4 + c
                        tok = slice(c * 128, (c + 1) * 128)
                        a = Aa[:, c, :]
                        XT = XT_[i % 2]
                        BT = BT_[i % 2]
                        cbm = cbm_[i % 2]
                        arhs = arhs_[i % 2]
                        acs = acs_[i % 2]
                        eacs = eacs_[i % 2]
                        etot = etot_[i % 2]
                        wend = wend_[i % 2]
                        segs = segs_[i % 2]
                        MT = MT_[i % 2]
                        XDT = XDT_[i % 2]
                        XW = XW_[i % 2]
                        t1 = t1_[i % 2]
                        t2 = t2_[i % 2]
                        ssq = ssq_[i % 2]
                        GN = GN_[i % 2]
                        junk = t2
                        for j in range(5):
                            self.tr(self.pT[:, j * 128:(j + 1) * 128], XB[:, j, tok])
                        self.cp("act", XT, self.pT[:, 0:512])
                        self.cp("dve", BT, self.pT[:, 512:640])
                        self.ck(4)
                        self.mm(pmisc[:, 128:256], XB[:, 4, tok], XB[:, 5, tok])
                        self.tt("dve", cbm, pmisc[:, 128:256], self.tri, ALU.mult)
                        self.tt("dve", arhs, V(self.tri.ap.unsqueeze(1).to_broadcast([128, 8, 128]), self.tri.bufs),
                                V(a.ap.unsqueeze(2).to_broadcast([128, 8, 128]), a.bufs), ALU.mult)
                        self.mm(pmisc[:, 8:16], self.tri, a)
                        self.mm(pmisc[:, 16:24], self.ones_f, a)
                        self.cp("act", acs, pmisc[:, 8:16])
                        self.act(eacs, pmisc[:, 8:16], AF.Exp)
                        self.act(etot, pmisc[:, 16:24], AF.Exp)
                        self.tt("dve", wend, pmisc[:, 16:24], acs, ALU.subtract)
                        self.act(wend, wend, AF.Exp)
                        self.ck(5)
                        for hh in range(2):
                            self.mm(pseg, self.ones_f, arhs[:, hh * 4:(hh + 1) * 4, :].re("p a b -> p (a b)"), start=True, stop=False)
                            self.mm(pseg, self.identb, self.negm4.re("p a b -> p (a b)"), start=False, stop=True)
                            self.tt("dve", segs, pseg.re("p (a b) -> p a b", a=4),
                                    V(acs.ap[:, hh * 4:(hh + 1) * 4].unsqueeze(2).to_broadcast([128, 4, 128]), acs.bufs), ALU.subtract)
                            self.act(segs, segs, AF.Exp)
                            self.tt("pool", MT[:, hh * 4:(hh + 1) * 4, :], segs,
                                    V(cbm.ap.unsqueeze(1).to_broadcast([128, 4, 128]), cbm.bufs), ALU.mult)
                        self.ck(6)
                        dtc = DT[:, c, gs]
                        XT3 = XT.re("p (r q) -> p r q", r=8)
                        self.tt("dve", XDT.re("p (r q) -> p r q", r=8), XT3,
                                V(dtc.ap.unsqueeze(2).to_broadcast([128, 8, 64]), dtc.bufs), ALU.mult)
                        self.tt("pool", XW.re("p (r q) -> p r q", r=8), XDT.re("p (r q) -> p r q", r=8),
                                V(wend.ap.unsqueeze(2).to_broadcast([128, 8, 64]), wend.bufs), ALU.mult)
                        for r in range(8):
                            self.mm(pyd[:, r * 64:(r + 1) * 64], MT[:, r, :], XDT[:, r * 64:(r + 1) * 64])
                        self.mm(pyo, XB[:, 5, tok], Sb)
                        self.mm(pds, BT, XW)
                        self.ck(7)
                        self.tt("dve", t1.re("p (r q) -> p r q", r=8), pyo.re("p (r q) -> p r q", r=8),
                                V(eacs.ap.unsqueeze(2).to_broadcast([128, 8, 64]), eacs.bufs), ALU.mult)
                        self.tt("dve", t1, pyd, t1, ALU.add)
                        self.tt("pool", t2.re("p (r q) -> p r q", r=8), XT3,
                                V(dsk.ap[:, g * 8:(g + 1) * 8].unsqueeze(2).to_broadcast([128, 8, 64]), dsk.bufs), ALU.mult)
                        self.tt("pool", t1, t1, t2, ALU.add)
                        self.tt("dve", S.re("p (r q) -> p r q", r=8), S.re("p (r q) -> p r q", r=8),
                                V(etot.ap.unsqueeze(2).to_broadcast([128, 8, 64]), etot.bufs), ALU.mult)
                        self.tt("dve", S, pds, S, ALU.add)
                        self.cp("act", Sb, S)
                        self.ck(8)
                        self.tt("dve", t1, t1, SZ[:, c, :], ALU.mult)
                        self.act(junk, t1, AF.Square, accum=ssq)
                        self.rstd(ssq, 512)
                        self.stt("dve", GN, t1, ssq, nwr, ALU.mult, ALU.mult)
                        self.ck(9)
                        self.store_yT(GN, i, g)
                        self.ck(10)

    def phaseB_ml(self):
        P = self.P
        with P.scope():
            wv = self.m_win.ap.rearrange("(k p) n -> p k n", p=128)
            Wqk = P.sb("Wqk", [128, 16, 512], BF16)
            Wv = P.sb("Wv", [128, 16, 512], BF16)
            Wo = P.sb("Wo", [128, 16, 512], BF16)
            Wzz = P.sb("Wzz", [128, 16, 512], BF16)
            Wg = P.sb("WgA", [128, 16, 8], BF16)
            hTs = [P.sb("hTs%d" % i, [128, 16, 512], BF16) for i in range(2)]
            cw = P.sb("cw", [128, 16, 4], F32)
            cb = P.sb("cb", [128, 16], F32)
            P.dma("sp", cw, self.m_cw)
            P.dma("sp", cb, self.m_cb)
            ibr = self.rowload("ibr", self.m_ib.ap, 4, self.m_ib)
            self.ts("dve", ibr, ibr, float(np.log(1.0 / 16.0)), None, ALU.add)
            nfb = self.rowload("nfb", self.m_fb.ap, 4, self.m_fb)
            self.ts("dve", nfb, nfb, -1.0, None, ALU.mult)
            nwr = P.sb("nwr", [128, 512], F32)
            U = [P.sb("U%d" % j, [128, 515], F32) for j in range(3)]
            self.Uh = [P.sb("Uh%d" % j, [128, 3], F32) for j in range(4)]
            self.cacc = [P.sb("cacc%d" % j, [128, 512], F32) for j in range(2)]
            QK_ = [P.sb("QK%d" % q_, [128, 4, 512], BF16) for q_ in range(2)]
            Vv_ = [P.sb("Vv%d" % q_, [128, 4, 512], BF16) for q_ in range(2)]
            SO_ = [P.sb("SO", [128, 4, 512], F32)] * 2
            SZ_ = [P.sb("SZ", [128, 4, 512], F32)] * 2
            IG_ = [P.sb("IG%d" % q_, [128, 4, 4], F32) for q_ in range(self.NSC)]
            Aa_ = [P.sb("Aa%d" % q_, [128, 4, 4], F32) for q_ in range(self.NSC)]
            KT_ = [P.sb("KT%d" % q_, [128, 256], BF16) for q_ in range(2)]
            sm_ = [P.sb("sm%d" % q_, [128, 128], F32) for q_ in range(2)]
            arhs_ = [P.sb("arhs%d" % q_, [128, 128], F32) for q_ in range(2)]
            bcum_ = [P.sb("bcum%d" % q_, [128, 1], F32) for q_ in range(2)]
            ebc_ = [P.sb("ebc%d" % q_, [128, 1], F32) for q_ in range(2)]
            etot_ = [P.sb("etot%d" % q_, [128, 1], F32) for q_ in range(2)]
            w2_ = [P.sb("w2%d" % q_, [128, 1], F32) for q_ in range(2)]
            w2b_ = [P.sb("w2b%d" % q_, [128, 1], BF16) for q_ in range(2)]
            segs_ = [P.sb("segs%d" % q_, [128, 128], F32) for q_ in range(2)]
            PT_ = [P.sb("PT%d" % q_, [128, 128], BF16) for q_ in range(2)]
            VW_ = [P.sb("VW%d" % q_, [128, 512], BF16) for q_ in range(2)]
            C = P.sb("C", [128, 2, 512], F32)
            Cb = P.sb("Cb", [128, 2, 512], BF16)
            nst = P.sb("nst", [128, 2], F32)
            nb = P.sb("nb", [128, 2], BF16)
            den_ = [P.sb("den%d" % q_, [128, 1], F32) for q_ in range(2)]
            rinv_ = [P.sb("rinv%d" % q_, [128, 1], F32) for q_ in range(2)]
            c2_ = [P.sb("c2%d" % q_, [128, 1], F32) for q_ in range(2)]
            t1_ = [P.sb("t1%d" % q_, [128, 512], F32) for q_ in range(2)]
            t2_ = [P.sb("t2%d" % q_, [128, 512], F32) for q_ in range(2)]
            junk_ = [P.sb("junkB%d" % q_, [128, 512], F32) for q_ in range(2)]
            ssq_ = [P.sb("ssq%d" % q_, [128, 1], F32) for q_ in range(2)]
            GN_ = [P.sb("GN%d" % q_, [128, 512], BF16) for q_ in range(2)]
            self.yst = [P.sb("yst%d" % i, [128, 4, 128], BF16) for i in range(2)]
            pmisc, pyd, pyo = self.pb[3], self.pb[5], self.pb[6]
            ptms = [self.pb[2], self.pb[4]]
            pseg = self.pb[3][:, 256:384]
            pdc = [self.pb[0], self.pb[1]]
            pg3 = pmisc[:, 40:48].re("p (h t) -> p h t", t=2)
            for h_ in range(4):
                P.dma("pool", Wg[:, :, 2 * h_:2 * h_ + 2], V(wv[:, :, h_ * 2050 + 2048:h_ * 2050 + 2050], self.m_win.bufs))
            for h in range(4):
                o = h * 2050
                P.dma("pool", Wqk, V(wv[:, :, o:o + 512], self.m_win.bufs))
                P.dma("pool", Wv, V(wv[:, :, o + 512:o + 1024], self.m_win.bufs))
                P.dma("pool", Wo, V(wv[:, :, o + 1024:o + 1536], self.m_win.bufs))
                P.dma("pool", Wzz, V(wv[:, :, o + 1536:o + 2048], self.m_win.bufs))
                P.dma("sp", nwr, V(self.m_nw.ap[:, h * 512:(h + 1) * 512].partition_broadcast(128), self.m_nw.bufs))
                P.op("pool", lambda e: e.memset(C.ap, 0.0), w=[C])
                P.op("pool", lambda e: e.memset(Cb.ap, 0.0), w=[Cb])
                P.op("pool", lambda e: e.memset(nst.ap, 0.0), w=[nst])
                P.op("pool", lambda e: e.memset(nb.ap, 0.0), w=[nb])
                wl = [Wqk[:, :, j * 128:(j + 1) * 128] for j in range(4)]
                cidx = [2 * h, 2 * h + 1, 8 + 2 * h, 8 + 2 * h + 1]
                for sc in range(self.NSC):
                    hs = hTs[sc % 2]
                    QK = QK_[sc % 2]
                    Vv = Vv_[sc % 2]
                    SO = SO_[sc % 2]
                    SZ = SZ_[sc % 2]
                    IG = IG_[sc]
                    Aa = Aa_[sc]
                    P.dma("sp", hs, V(self.hT_d[sc], [self.hT_bufs[sc]]))
                    self.conv_fm(hs, wl, U, QK, cw, cb, cidx, sc == 0)
                    for c in range(4):
                        tok = slice(c * 128, (c + 1) * 128)
                        for wi_, (W_, dst, fn) in enumerate(((Wv, Vv, None), (Wo, SO, AF.Sigmoid), (Wzz, SZ, AF.Silu))):
                            ptm = ptms[(c * 3 + wi_) % 2]
                            for k in range(16):
                                self.mm(ptm, hs[:, k, tok], W_[:, k, :], start=(k == 0), stop=(k == 15))
                            if fn is None:
                                self.cp("act", dst[:, c, :], ptm)
                            else:
                                self.act(dst[:, c, :], ptm, fn)
                        if h == 0:
                            for k in range(16):
                                self.mm(pmisc[:, 40:48], hs[:, k, tok], Wg[:, k, :], start=(k == 0), stop=(k == 15))
                            self.tt("dve", IG[:, c, :], pg3[:, :, 0], ibr, ALU.add)
                            self.stt("dve", Aa[:, c, :], pg3[:, :, 1], -1.0, nfb, ALU.mult, ALU.add)
                            self.act(Aa[:, c, :], Aa[:, c, :], AF.Exp)
                            self.act(Aa[:, c, :], Aa[:, c, :], AF.Ln, bias=1.0)
                            self.ts("dve", Aa[:, c, :], Aa[:, c, :], -1.0, None, ALU.mult)
                    for c in range(4):
                        i = sc * 4 + c
                        tok = slice(c * 128, (c + 1) * 128)
                        a = Aa[:, c, h:h + 1]
                        KT = KT_[i % 2]
                        sm = sm_[i % 2]
                        arhs = arhs_[i % 2]
                        bcum = bcum_[i % 2]
                        ebc = ebc_[i % 2]
                        etot = etot_[i % 2]
                        w2 = w2_[i % 2]
                        w2b = w2b_[i % 2]
                        segs = segs_[i % 2]
                        PT = PT_[i % 2]
                        VW = VW_[i % 2]
                        den = den_[i % 2]
                        rinv = rinv_[i % 2]
                        c2 = c2_[i % 2]
                        t1 = t1_[i % 2]
                        t2 = t2_[i % 2]
                        junk = junk_[i % 2]
                        ssq = ssq_[i % 2]
                        GN = GN_[i % 2]
                        ig = IG[:, c, h:h + 1]
                        for j in range(2):
                            self.tr(self.pT[:, j * 128:(j + 1) * 128], QK[:, 2 + j, tok])
                        self.cp("act", KT, self.pT[:, 0:256])
                        for j in range(2):
                            self.mm(pmisc[:, 128:256], QK[:, 2 + j, tok], QK[:, j, tok], start=(j == 0), stop=(j == 1))
                        self.tt("dve", sm, pmisc[:, 128:256], self.tri, ALU.mult)
                        self.ts("dve", arhs, self.tri, a, None, ALU.mult)
                        self.mm(pmisc[:, 8:9], self.tri, a)
                        self.mm(pmisc[:, 16:17], self.ones_f, a)
                        self.cp("act", bcum, pmisc[:, 8:9])
                        self.act(ebc, pmisc[:, 8:9], AF.Exp)
                        self.act(etot, pmisc[:, 16:17], AF.Exp)
                        self.tt("dve", w2, pmisc[:, 16:17], bcum, ALU.subtract)
                        self.act(w2, w2, AF.Exp, bias=ig)
                        self.cp("dve", w2b, w2)
                        self.mm(pseg, self.ones_f, arhs, start=True, stop=False)
                        self.mm(pseg, self.identb, self.negm4[:, 0, :], start=False, stop=True)
                        self.ts("dve", segs, pseg, bcum, None, ALU.subtract)
                        self.act(segs, segs, AF.Exp, bias=ig)
                        self.tt("pool", PT, segs, sm, ALU.mult)
                        self.ts("dve", VW, Vv[:, c, :], w2, None, ALU.mult)
                        self.mm(pyd, PT, Vv[:, c, :])
                        for j in range(2):
                            self.mm(pyo, QK[:, j, tok], Cb[:, j, :], start=(j == 0), stop=(j == 1))
                        self.mm(pmisc[:, 24:25], PT, self.ones_b)
                        for j in range(2):
                            self.mm(pmisc[:, 25:26], QK[:, j, tok], nb[:, j:j + 1], start=(j == 0), stop=(j == 1))
                        for j in range(2):
                            self.mm(pdc[j], KT[:, j * 128:(j + 1) * 128], VW)
                            self.mm(pmisc[:, 32 + j:33 + j], KT[:, j * 128:(j + 1) * 128], w2b)
                        self.tt("dve", den, pmisc[:, 25:26], ebc, ALU.mult)
                        self.tt("dve", den, pmisc[:, 24:25], den, ALU.add)
                        self.ts("dve", c2, den, -1.0, None, ALU.mult)
                        self.tt("dve", den, den, c2, ALU.max)
                        self.ts("dve", den, den, 1.0, None, ALU.max)
                        self.P.op("dve", lambda e, o_=rinv.ap, i_=den.ap: e.reciprocal(o_, i_), r=[den], w=[rinv])
                        self.tt("dve", c2, ebc, rinv, ALU.mult)
                        self.ts("dve", t2, pyo, c2, None, ALU.mult)
                        self.stt("dve", t1, pyd, rinv, t2, ALU.mult, ALU.add)
                        for j in range(2):
                            self.stt("dve", C[:, j, :], C[:, j, :], etot, pdc[j], ALU.mult, ALU.add)
                            self.cp("act", Cb[:, j, :], C[:, j, :])
                        self.stt("dve", nst, nst, etot, pmisc[:, 32:34], ALU.mult, ALU.add)
                        self.cp("dve", nb, nst)
                        self.tt("pool", t1, t1, SO[:, c, :], ALU.mult)
                        self.act(junk, t1, AF.Square, accum=ssq)
                        self.rstd(ssq, 512)
                        self.stt("dve", t1, t1, ssq, nwr, ALU.mult, ALU.mult)
                        self.tt("pool", GN, t1, SZ[:, c, :], ALU.mult)
                        self.store_yT(GN, i, h)

    def phaseC(self, wout, resf, l, mod, next_mod=False, tail=None):
        P = self.P
        R = self.NST * 128
        with P.scope():
            if next_mod:
                mod2 = P.sb("mod2", [128, 5120], F32)
                self.phase0_body(l, mod2, 4)
            wvv = wout.ap.rearrange("(c p) n -> p c n", p=128)
            Whs = [P.sb("Wh%d" % q, [128, 8, 1024], BF16) for q in range(4)]
            ys = [P.sb("ys%d" % i, [128, 32, 128], BF16) for i in range(2 if next_mod else 3)]
            xt = [P.sb("xtC%d" % i, [128, 1024], F32) for i in range(2)]
            xo = [P.sb("xoC%d" % i, [128, 1024], F32) for i in range(2)]
            for q in range(4):
                P.dma("pool", Whs[q], V(wvv[:, q * 8:(q + 1) * 8, :], wout.bufs))
            for i in range(self.NST):
                y_ = ys[i % len(ys)]
                pt, off = i // self.NY, (i % self.NY) * 128
                for rk in range(2):
                    src = self.yT_all[:, pt, rk * self.RY + off: rk * self.RY + off + 128, :].rearrange("g p n -> p g n")
                    P.dma("sp", y_[:, rk * 16:(rk + 1) * 16, :].re("p (g q) t -> p g (q t)", g=4),
                          V(src, [self.yT_all_bufs[g_][pt] for g_ in range(4)]), nbytes=128 * 4 * 512 * 2)
                x_ = xt[i % 2]
                P.dma("sp", x_, resf(i))
                o_ = xo[i % 2]
                for n2 in range(2):
                    ps = self.pb[(i * 2 + n2) % 4]
                    for c in range(32):
                        self.mm(ps, y_[:, c, :], Whs[c // 8][:, c % 8, n2 * 512:(n2 + 1) * 512], start=(c == 0), stop=(c == 31))
                    col = slice(2 * D + n2 * 512, 2 * D + (n2 + 1) * 512)
                    self.tt("dve", o_[:, n2 * 512:(n2 + 1) * 512], ps, mod[:, col], ALU.mult)
                    self.tt("pool", o_[:, n2 * 512:(n2 + 1) * 512], o_[:, n2 * 512:(n2 + 1) * 512],
                            x_[:, n2 * 512:(n2 + 1) * 512], ALU.add)
                q, off = i // self.NQ, (i % self.NQ) * 128
                self.store(V(self.xl[l][q, off:off + 128, :], [self.xl_bufs[l][q]]), o_)
                if i % self.NQ == self.NQ - 1:
                    self.allgather(self.xl[l][q], self.xl_bufs[l][q], self.xa[l][q], self.xa_bufs[l][q],
                                   2 * self.NQ * 128 * 1024 * 4)
            if next_mod:
                self.cp("dve", mod, mod2)
            if tail is not None:
                tail()

    def res_xh(self, i):
        return V(self.xh.ap[i * 128:(i + 1) * 128, :], self.xh.bufs)

    def res_loc(self, l):
        def f(i):
            q, off = i // self.NQ, (i % self.NQ) * 128
            return V(self.xl[l][q, off:off + 128, :], [self.xl_bufs[l][q]])
        return f

    def phaseD(self, srcf):
        with self.P.scope():
            self.phaseD_body(srcf)

    def phaseD_body(self, srcf):
        P = self.P
        if True:
            fw_ = self.rowload("fnw", self.fnorm_w.ap, D, self.fnorm_w)
            xt = [P.sb("xtD%d" % i, [128, D], F32) for i in range(2)]
            ot = [P.sb("otD%d" % i, [128, D], F32) for i in range(2)]
            junk = P.sb("junkD", [128, D], BF16)
            ss = [P.sb("ssD%d" % i, [128, 1], F32) for i in range(2)]
            for i in range(self.NST):
                x_ = xt[i % 2]
                P.dma("sp", x_.re("p (r n) -> p r n", r=2), srcf(i))
                s_ = ss[i % 2]
                self.act(junk, x_, AF.Square, accum=s_)
                self.rstd(s_, D)
                self.stt("dve", ot[i % 2], x_, s_, fw_, ALU.mult, ALU.mult)
                self.store(V(self.out[i * 128:(i + 1) * 128, :], [self.out_bufs[i]]), ot[i % 2])

    def build(self, upto="all"):
        P = self.P
        mod = P.sb("mod", [128, 5120], F32)
        self.phaseA(self.src_x, mod, phase0_l=0)
        self.phaseB_ssd()
        self.phaseC(self.s_wout, self.res_xh, 1, mod, next_mod=(upto != "l1"))
        if upto == "l1":
            self.phaseD(self.src_all(1))
        else:
            self.phaseA(self.src_all(1), mod)
            self.phaseB_ml()
            self.phaseC(self.m_wout, self.res_loc(1), 2, mod, tail=lambda: self.phaseD_body(self.src_all(2)))
        P.op("sp", lambda e: e.nop(), r=[V(self.out, self.out_bufs)])
        P.emit(sched=self.sched)
        print("sim_time_ms", getattr(P, "sim_time", 0) / 1e6, {e: round(b / 1e6, 2) for e, b in getattr(P, "sim_busy", {}).items()})
        return self.nc


def _shared_inputs(inp, r):
    f = np.float32
    ii = np.eye(128, dtype=f)
    tri = np.triu(np.ones((128, 128), f))
    negm = np.where(np.arange(128)[None, :] < np.arange(128)[:, None], f(NEG), f(0)).astype(f)
    consts = np.stack([ii, tri, negm, np.ones((128, 128), f)], axis=1)
    ar = np.arange
    scols, sch = [], []
    for gl in range(4):
        g = 4 * r + gl
        scols += [g * 512 + ar(512), 4096 + g * 512 + ar(512), 8192 + g * 128 + ar(128), 9216 + g * 128 + ar(128),
                  10240 + g * 8 + ar(8)]
    scols = np.concatenate(scols)
    sch = [4 * (4 * r + gl) + j for gl in range(4) for j in range(4)] + [32 + 4 * r + gl for gl in range(4)] + \
          [40 + 4 * r + gl for gl in range(4)]
    mcols = []
    for hl in range(4):
        h = 4 * r + hl
        mcols += [h * 256 + ar(256), 2048 + h * 256 + ar(256), 4096 + h * 512 + ar(512), 8192 + h * 512 + ar(512),
                  12288 + h * 512 + ar(512), np.array([16384 + h, 16392 + h])]
    mcols = np.concatenate(mcols)
    mch = [2 * (4 * r + hl) + j for hl in range(4) for j in range(2)] + [16 + 2 * (4 * r + hl) + j for hl in range(4) for j in range(2)]
    acols = np.concatenate([ar(4096), 4096 + r * 1024 + ar(1024)])
    cw = lambda w: np.ascontiguousarray(w.T.reshape(-1, 128, 4).transpose(1, 0, 2))
    cbv = lambda v: np.ascontiguousarray(v.reshape(-1, 128).T)
    osl = slice(r * 1024, (r + 1) * 1024)
    hs32 = slice(32 * r, 32 * r + 32)
    return {
        "norm_w": inp["norm_w"], "final_norm_w": inp["final_norm_w"].reshape(1, -1),
        "ada_w": np.ascontiguousarray(inp["ada_w"][:, :, acols]), "ada_b": np.ascontiguousarray(inp["ada_b"][:, acols]),
        "ssd_w_in": np.ascontiguousarray(inp["ssd_w_in"][0][:, scols]),
        "ssd_w_out": np.ascontiguousarray(inp["ssd_w_out"][0][:, osl]),
        "ssd_cw": np.ascontiguousarray(cw(inp["ssd_conv_w"][0])[:, sch, :]),
        "ssd_cb": np.ascontiguousarray(cbv(inp["ssd_conv_b"][0])[:, sch]),
        "ssd_dt_bias": np.ascontiguousarray(inp["ssd_dt_bias"][:, hs32]),
        "ssd_a_log": np.ascontiguousarray(inp["ssd_a_log"][:, hs32]),
        "ssd_d": np.ascontiguousarray(inp["ssd_d"][:, hs32]),
        "ssd_norm_w": np.ascontiguousarray(inp["ssd_norm_w"][:, r * 2048:(r + 1) * 2048]),
        "ml_w_in": np.ascontiguousarray(inp["ml_w_in"][0][:, mcols]),
        "ml_w_out": np.ascontiguousarray(inp["ml_w_out"][0][:, osl]),
        "ml_cw": np.ascontiguousarray(cw(inp["ml_conv_w"][0])[:, mch, :]),
        "ml_cb": np.ascontiguousarray(cbv(inp["ml_conv_b"][0])[:, mch]),
        "ml_igate_b": np.ascontiguousarray(inp["ml_igate_b"][:, 4 * r:4 * r + 4]),
        "ml_fgate_b": np.ascontiguousarray(inp["ml_fgate_b"][:, 4 * r:4 * r + 4]),
        "ml_norm_w": np.ascontiguousarray(inp["ml_norm_w"][:, r * 2048:(r + 1) * 2048]),
        "consts": consts,
    }


def host_inputs(inp, b, r, shared):
    d = dict(shared[r])
    d["x"] = np.ascontiguousarray(inp["x"][b])
    d["x_half"] = np.ascontiguousarray(inp["x"][b][:, r * 1024:(r + 1) * 1024])
    d["cT"] = np.ascontiguousarray(inp["c"][b].reshape(128, 16))
    return d


_NC_CACHE = {}


def run(inp, upto="all", ncores=8, cut=None):
    inp = {k: np.asarray(v, dtype=np.float32) for k, v in inp.items()}
    B, T, _ = inp["x"].shape
    key = (T, upto, cut)
    if key not in _NC_CACHE:
        kb = KB(T)
        kb.cut = cut
        _NC_CACHE[key] = kb.build(upto)
    nc = _NC_CACHE[key]
    shared = [_shared_inputs(inp, r) for r in range(2)]
    in_maps = [host_inputs(inp, (i // 2) % B, i % 2, shared) for i in range(ncores)]
    res = run_bass_kernel_spmd(nc, in_maps, core_ids=list(range(ncores)))
    return np.stack([res.results[2 * b]["out"] for b in range(B)], axis=0)


def kernel(**inputs):
    return run(inputs, "all", 8).astype(np.float32)
```

```python
from contextlib import ExitStack
import numpy as np
import concourse.bass as bass
import concourse.mybir as mybir
from concourse.bass_utils import run_bass_kernel_spmd

F32 = mybir.dt.float32
BF16 = mybir.dt.bfloat16
AF = mybir.ActivationFunctionType
ALU = mybir.AluOpType
AX = mybir.AxisListType


class Buf:
    __slots__ = ("name", "writers", "readers", "sem", "dcnt", "excl")

    def __init__(self, name, excl=False):
        self.name = name
        self.excl = excl
        self.writers = []
        self.readers = []
        self.sem = None
        self.dcnt = 0


class V:
    __slots__ = ("ap", "bufs")

    def __init__(self, ap, bufs):
        self.ap = ap
        self.bufs = tuple(bufs)

    def __getitem__(self, k):
        return V(self.ap[k], self.bufs)

    def re(self, s, **kw):
        return V(self.ap.rearrange(s, **kw), self.bufs)

    def bc(self, shape):
        return V(self.ap.to_broadcast(shape), self.bufs)


class Op:
    __slots__ = ("eng", "fn", "deps", "odeps", "signal", "val", "sem", "is_dma", "name", "inc", "cost", "lat", "epoch", "seq", "fin", "npred", "succ", "rt", "st", "crit", "line")


class Prog:
    ENGS = ["pe", "act", "dve", "pool", "sp"]
    BLK = {"pe": "tensor", "act": "scalar", "dve": "vector", "pool": "gpsimd", "sp": "sync"}

    def __init__(self, nc):
        self.nc = nc
        self.ops = {e: [] for e in self.ENGS}
        self.stack = ExitStack()
        self.dma_bufs = []
        self.nsb = 0
        self.all_dma_out = []
        self.pending_dma = []
        self.stacks = [self.stack]
        self.bar_scr = None
        self.epoch = 0
        self.nseq = 0
        self.debug_lines = False

    def sb(self, name, shape, dtype):
        self.nsb += 1
        t = self.stacks[-1].enter_context(self.nc.sbuf_tensor("%s_%d" % (name, self.nsb), list(shape), dtype))
        return V(t.ap(), [Buf(name)])

    def ps(self, name, shape, dtype=F32):
        t = self.stack.enter_context(self.nc.psum_tensor(name, list(shape), dtype))
        return V(t.ap(), [Buf(name, excl=True)])

    def dram(self, name, shape, dtype, kind="Internal"):
        t = self.nc.dram_tensor(name, list(shape), dtype, kind=kind)
        return V(t.ap(), [Buf(name)])

    def op(self, eng, fn, r=(), w=(), dma=False, key=None, name=None, extra=(), cost=100.0, lat=0.0, inc=16):
        op = Op()
        op.inc = inc
        op.eng, op.fn, op.signal, op.val, op.sem, op.is_dma, op.name = eng, fn, False, 0, None, dma, name
        op.deps = []
        op.odeps = []
        op.cost, op.lat, op.epoch = cost, lat, self.epoch
        self.nseq += 1
        op.seq = self.nseq
        op.crit = None
        if self.debug_lines:
            import sys as _s
            f = _s._getframe(1)
            while f.f_code.co_name in ("op", "dma", "mm", "tt", "ts", "stt", "act", "cp", "tr", "store", "rstd", "rowload", "allgather"):
                f = f.f_back
            op.line = f.f_lineno
        else:
            op.line = 0
        rb = [b for v in r for b in v.bufs]
        wb = [b for v in w for b in v.bufs]
        cand = []
        for b in rb:
            cand += b.writers
            if b.excl:
                cand += [x for x in b.readers if x.eng != eng]
        for b in wb:
            cand += b.writers
            cand += b.readers
        cand += list(extra)
        seen = set()
        for d in cand:
            if d is op or id(d) in seen:
                continue
            seen.add(id(d))
            if (not d.is_dma) and (not dma) and d.eng == eng and eng == "pe":
                op.odeps.append(d)
                continue
            d.signal = True
            op.deps.append(d)
        for b in wb:
            if b.readers:
                b.writers = [op]
                b.readers = []
            else:
                b.writers = [x for x in b.writers if not (x.eng == eng and x.is_dma == dma and not dma)] + [op]
        for b in rb:
            b.readers.append(op)
        if dma:
            kb = key if key is not None else (wb[0] if wb else rb[0])
            if kb.sem is None:
                self.dma_bufs.append(kb)
                kb.sem = "pending"
            kb.dcnt += 1
            op.sem = kb
            op.val = inc * kb.dcnt
            op.signal = True
            if inc == 16:
                self.pending_dma.append(op)
        self.ops[eng].append(op)
        return op

    def barrier(self):
        if self.bar_scr is None:
            self.bar_scr = {e: self.sb("bar_" + e, [128, 2], F32) for e in ("act", "dve", "pool")}
            for e in ("dve", "pool"):
                self.op(e, lambda g, a=self.bar_scr[e].ap: g.memset(a, 0.0), w=[self.bar_scr[e]])
            self.op("act", lambda g, a=self.bar_scr["act"].ap, b=self.bar_scr["dve"].ap: g.copy(a, b),
                    r=[self.bar_scr["dve"]], w=[self.bar_scr["act"]])
        self.epoch += 1
        arr = []
        for e in ("dve", "pool"):
            arr.append(self.op(e, lambda g, a=self.bar_scr[e].ap: g.memset(a[:, 0:1], 0.0), w=[self.bar_scr[e]]))
        arr.append(self.op("act", lambda g, a=self.bar_scr["act"].ap: g.copy(a[:, 0:1], a[:, 1:2]),
                           w=[self.bar_scr["act"]]))
        pend = list(self.pending_dma)
        self.pending_dma = []
        for e in self.ENGS:
            self.op(e, lambda g: g.nop(), extra=arr + pend, name="bar")
        self.epoch += 1

    def scope(self):
        return _Scope(self)

    def dma(self, eng, out, in_, key=None, nbytes=None, **kw):
        if nbytes is None:
            n = 1
            for d in out.ap.shape:
                n *= d
            nbytes = n * 4
        o = self.op(eng, lambda e: e.dma_start(out=out.ap, in_=in_.ap, **kw), r=[in_], w=[out], dma=True,
                    key=key, cost=(60.0 if eng == "sp" else 300.0 + nbytes / 2000.0), lat=2000.0 + nbytes / 100.0)
        return o

    def schedule(self):
        import heapq
        allops = [o for e in self.ENGS for o in self.ops[e]]
        allops.sort(key=lambda o: o.seq)
        for o in allops:
            o.succ = []
            o.npred = 0
            o.rt = 0.0
            o.fin = 0.0
        for o in allops:
            for d in o.deps + o.odeps:
                d.succ.append(o)
                o.npred += 1
        prio = {}
        for o in reversed(allops):
            m = 0.0
            for sx in o.succ:
                if sx.epoch == o.epoch:
                    p = prio[id(sx)]
                    if p > m:
                        m = p
            prio[id(o)] = o.cost + o.lat + m
        byep = {}
        for o in allops:
            byep.setdefault(o.epoch, []).append(o)
        tfree = {e: 0.0 for e in self.ENGS}
        busy = {e: 0.0 for e in self.ENGS}
        new = {e: [] for e in self.ENGS}
        tglob = 0.0
        for ep in sorted(byep):
            ops = byep[ep]
            wait = {e: [] for e in self.ENGS}
            ready = {e: [] for e in self.ENGS}
            left = len(ops)
            for o in ops:
                if o.npred == 0:
                    heapq.heappush(wait[o.eng], (o.rt, o.seq, o))
            while left:
                best = None
                for e in self.ENGS:
                    w, r = wait[e], ready[e]
                    while w and w[0][0] <= tfree[e]:
                        _, sq, o2 = heapq.heappop(w)
                        heapq.heappush(r, (-prio[id(o2)], sq, o2))
                    if r:
                        st = tfree[e]
                    elif w:
                        st = w[0][0]
                    else:
                        continue
                    if best is None or st < best[0]:
                        best = (st, e)
                assert best is not None, "scheduler deadlock (cross-epoch dependency?)"
                st, e = best
                if ready[e]:
                    _, _, o = heapq.heappop(ready[e])
                else:
                    _, _, o = heapq.heappop(wait[e])
                tfree[e] = st + o.cost
                o.st = st
                busy[e] += o.cost
                o.fin = st + o.cost + o.lat
                new[e].append(o)
                left -= 1
                for sx in o.succ:
                    lat = 60.0 if (sx.eng == o.eng and not o.is_dma) else 250.0
                    if sx.eng == "pe" and o.eng == "pe":
                        lat = 0.0
                    if o.fin + lat > sx.rt:
                        sx.rt = o.fin + lat
                        sx.crit = o
                    sx.npred -= 1
                    if sx.npred == 0 and sx.epoch == ep:
                        heapq.heappush(wait[sx.eng], (sx.rt, sx.seq, sx))
            t_prev = tglob
            tmax = max([tfree[e] for e in self.ENGS] + [o.fin for o in ops])
            tglob = tmax
            if len(ops) > 50:
                eb = {}
                for o in ops:
                    eb[o.eng] = eb.get(o.eng, 0.0) + o.cost
                print("  epoch %d: %.3f ms  n=%d  busy=%s" % (ep, (tmax - t_prev) / 1e6, len(ops),
                      {e: round(v / 1e6, 2) for e, v in eb.items()}))
            for e in self.ENGS:
                tfree[e] = tmax
        self.ops = new
        self.sim_time = max(tfree.values())
        self.sim_busy = busy

    def emit(self, final_wait_ops=(), sched=True):
        nc = self.nc
        st = self.stack
        if sched:
            self.schedule()
        esem = {e: st.enter_context(nc.semaphore("s_" + e)) for e in self.ENGS}
        for i, b in enumerate(self.dma_bufs):
            b.sem = st.enter_context(nc.semaphore("d%d" % i))
        for e in self.ENGS:
            c = 0
            for op in self.ops[e]:
                if op.is_dma:
                    op.sem = op.sem.sem if isinstance(op.sem, Buf) else op.sem
                elif op.signal:
                    c += 1
                    op.val = c
                    op.sem = esem[e]
        self.counts = {e: sum(1 for o in self.ops[e]) for e in self.ENGS}
        block = st.enter_context(nc.Block())
        for e in self.ENGS:
            ops = self.ops[e]
            if not ops:
                continue

            def body(eng, ops=ops):
                waited = {}
                for op in ops:
                    need = {}
                    for d in op.deps:
                        k = id(d.sem)
                        if k not in need or need[k][1] < d.val:
                            need[k] = (d.sem, d.val)
                    for k, (s, v) in need.items():
                        if waited.get(k, 0) < v:
                            eng.wait_ge(s, v)
                            waited[k] = v
                    ins = op.fn(eng)
                    if op.signal:
                        ins.then_inc(op.sem, op.inc if op.is_dma else 1)

            getattr(block, self.BLK[e])(body)

    def close(self):
        self.stack.close()


class _Scope:
    def __init__(self, P):
        self.P = P

    def __enter__(self):
        self.st = ExitStack()
        self.P.stacks.append(self.st)
        return self

    def __exit__(self, *a):
        self.P.barrier()
        self.P.stacks.pop()
        self.st.close()
        return False


def gap_report(P, epoch, eng="pe", top=12):
    ops = [o for o in P.ops[eng] if o.epoch == epoch]
    agg = {}
    prev_end = ops[0].st
    for o in ops:
        gap = o.st - prev_end
        if gap > 1.0:
            c = o.crit
            key = (o.line, c.eng if c else None, c.line if c else None, c.is_dma if c else None)
            a = agg.setdefault(key, [0.0, 0])
            a[0] += gap
            a[1] += 1
        prev_end = o.st + o.cost
    tot = sum(a[0] for a in agg.values())
    print("gap report epoch %d eng %s: total gap %.3f ms" % (epoch, eng, tot / 1e6))
    for key, a in sorted(agg.items(), key=lambda kv: -kv[1][0])[:top]:
        print("   op@%s waits on %s@%s dma=%s : %.3f ms (%d)" % (key[0], key[1], key[2], key[3], a[0] / 1e6, a[1]))


D = 2048
KC = 16
EPS = 1e-6
NEG = -30000.0


class Cut(Exception):
    pass


class KB:
    cut = None
    sched = True

    def ck(self, n):
        if self.cut is not None and self.cut == n:
            raise Cut()

    def __init__(self, T, nlayers=2):
        self.T = T
        self.NSC = T // 512
        self.NST = T // 128
        nc = bass.Bass("TRN2", target_bir_lowering=False)
        self.nc = nc
        P = Prog(nc)
        self.P = P
        ei = lambda n, s, d=F32: P.dram(n, s, d, kind="ExternalInput")
        self.x = ei("x", [T, D])
        self.xh = ei("x_half", [T, 1024])
        self.cT = ei("cT", [128, 16])
        self.norm_w = ei("norm_w", [2, D])
        self.fnorm_w = ei("final_norm_w", [1, D])
        self.ada_w = ei("ada_w", [2, D, 5120])
        self.ada_b = ei("ada_b", [2, 5120])
        self.s_win = ei("ssd_w_in", [D, 5152])
        self.s_wout = ei("ssd_w_out", [4096, 1024])
        self.s_cw = ei("ssd_cw", [128, 24, 4])
        self.s_cb = ei("ssd_cb", [128, 24])
        self.s_dtb = ei("ssd_dt_bias", [1, 32])
        self.s_alog = ei("ssd_a_log", [1, 32])
        self.s_d = ei("ssd_d", [1, 32])
        self.s_nw = ei("ssd_norm_w", [1, 2048])
        self.m_win = ei("ml_w_in", [D, 8200])
        self.m_wout = ei("ml_w_out", [4096, 1024])
        self.m_cw = ei("ml_cw", [128, 16, 4])
        self.m_cb = ei("ml_cb", [128, 16])
        self.m_ib = ei("ml_igate_b", [1, 4])
        self.m_fb = ei("ml_fgate_b", [1, 4])
        self.m_nw = ei("ml_norm_w", [1, 2048])
        self.consts = ei("consts", [128, 4, 128])
        self.out = P.nc.dram_tensor("out", [T, D], F32, kind="ExternalOutput").ap()
        self.out_bufs = [Buf("out%d" % i) for i in range(self.NST)]
        self.hT_d = nc.dram_tensor("hT_d", [self.NSC, 128, 16, 512], BF16).ap()
        self.hT_bufs = [Buf("hT%d" % i) for i in range(self.NSC)]
        R = self.NST * 128
        self.YC = max(1, R // 2048)
        self.RY = R // self.YC
        self.NY = self.NST // self.YC
        self.yT_loc = nc.dram_tensor("yT_loc", [4, self.YC, self.RY, 512], BF16).ap()
        self.yT_loc_bufs = [[Buf("yTl%d_%d" % (i, j)) for j in range(self.YC)] for i in range(4)]
        self.yT_all = nc.dram_tensor("yT_all", [4, self.YC, 2 * self.RY, 512], BF16).ap()
        self.yT_all_bufs = [[Buf("yTa%d_%d" % (i, j)) for j in range(self.YC)] for i in range(4)]
        self.NXC = max(4, T // 512)
        self.NQ = self.NST // self.NXC
        TQ = T // self.NXC
        self.xl, self.xa, self.xl_bufs, self.xa_bufs = {}, {}, {}, {}
        for l in (1, 2):
            self.xl[l] = nc.dram_tensor("x%d_loc" % l, [self.NXC, TQ, 1024], F32).ap()
            self.xa[l] = nc.dram_tensor("x%d_all" % l, [self.NXC, 2 * TQ, 1024], F32).ap()
            self.xl_bufs[l] = [Buf("x%dl%d" % (l, i)) for i in range(self.NXC)]
            self.xa_bufs[l] = [Buf("x%da%d" % (l, i)) for i in range(self.NXC)]
        self.cc_key = Buf("cc")
        self.xin_bufs = [self.x.bufs[0]] * self.NST
        self.pb = [P.ps("pb%d" % i, [128, 512], F32) for i in range(7)]
        self.pT = P.ps("pT", [128, 1024], BF16)
        self.pT2 = V(self.pb[0].ap.bitcast(BF16), self.pb[0].bufs)
        self.setup_consts()
        P.barrier()

    @staticmethod
    def fsz(v):
        n = 1
        for d in v.ap.shape[1:]:
            n *= d
        return n

    def ecost(self, eng, out, *ins):
        F = self.fsz(out)
        if eng == "dve":
            return 90.0 + F / 0.8
        if eng == "act":
            return 220.0 + F / 0.9
        return 120.0 + F / 0.5

    def mm(self, out, lhsT, rhs, start=True, stop=True):
        N = self.fsz(rhs)
        f32 = lhsT.ap.dtype == F32
        c = max(N, 64) / 2.4 * (4 if f32 else 1) + (180.0 if f32 else 90.0) + 10.0
        self.P.op("pe", lambda e: e.matmul(out.ap, lhsT.ap, rhs.ap, start=start, stop=stop), r=[lhsT, rhs], w=[out], cost=c)

    def tr(self, out, in_):
        idb = self.identb
        self.P.op("pe", lambda e: e.transpose(out.ap, in_.ap, idb.ap), r=[in_, idb], w=[out], cost=155.0)

    def tt(self, eng, out, in0, in1, op):
        self.P.op(eng, lambda e: e.tensor_tensor(out.ap, in0.ap, in1.ap, op), r=[in0, in1], w=[out],
                  cost=self.ecost(eng, out))

    def ts(self, eng, out, in0, s1, s2, op0, op1=None):
        r = [in0] + [s for s in (s1, s2) if isinstance(s, V)]
        a1 = s1.ap if isinstance(s1, V) else s1
        a2 = s2.ap if isinstance(s2, V) else s2
        c = self.ecost(eng, out)
        if op1 is None:
            self.P.op(eng, lambda e: e.tensor_single_scalar(out.ap, in0.ap, a1, op0), r=r, w=[out], cost=c)
        else:
            self.P.op(eng, lambda e: e.tensor_scalar(out.ap, in0.ap, a1, a2, op0, op1), r=r, w=[out], cost=c)

    def stt(self, eng, out, in0, sc, in1, op0, op1):
        r = [in0, in1] + ([sc] if isinstance(sc, V) else [])
        a = sc.ap if isinstance(sc, V) else sc
        self.P.op(eng, lambda e: e.scalar_tensor_tensor(out.ap, in0.ap, a, in1.ap, op0, op1), r=r, w=[out],
                  cost=self.ecost(eng, out))

    def act(self, out, in_, func, bias=None, scale=None, accum=None):
        r = [in_] + [s for s in (bias, scale) if isinstance(s, V)]
        w = [out] + ([accum] if accum is not None else [])
        kw = {}
        if bias is not None:
            kw["bias"] = bias.ap if isinstance(bias, V) else bias
        if scale is not None:
            kw["scale"] = scale.ap if isinstance(scale, V) else scale
        if accum is not None:
            kw["accum_out"] = accum.ap
        self.P.op("act", lambda e: e.activation(out.ap, in_.ap, func, **kw), r=r, w=w, cost=self.ecost("act", out))

    def cp(self, eng, out, in_):
        c = self.ecost(eng, out)
        if eng == "act":
            self.P.op("act", lambda e: e.copy(out.ap, in_.ap), r=[in_], w=[out], cost=c)
        else:
            self.P.op(eng, lambda e: e.tensor_copy(out.ap, in_.ap), r=[in_], w=[out], cost=c)

    def allgather(self, src_ap, src_buf, dst_ap, dst_buf, nbytes):
        rg = [[0, 1], [2, 3], [4, 5], [6, 7]]
        self.P.op("pool", lambda e: e.collective_compute("AllGather", ALU.bypass, replica_groups=rg,
                                                         ins=[src_ap.opt()], outs=[dst_ap.opt()]),
                  r=[V(src_ap, [src_buf])], w=[V(dst_ap, [dst_buf])], dma=True, inc=1, key=self.cc_key,
                  cost=2000.0, lat=30000.0 + nbytes / 60.0)

    def src_x(self, i):
        return V(self.x.ap[i * 128:(i + 1) * 128, :].rearrange("p (r n) -> p r n", r=2), self.x.bufs)

    def src_all(self, l):
        def f(i):
            q, o = i // self.NQ, (i % self.NQ) * 128
            ap = self.xa[l][q].rearrange("(r t) n -> t r n", r=2)[o:o + 128]
            return V(ap, [self.xa_bufs[l][q]])
        return f

    def store(self, dst, src):
        self.P.dma("sp", dst, src, key=src.bufs[0])

    def rowload(self, name, src_ap, n, src_v):
        t = self.P.sb(name, [128, n], F32)
        self.P.dma("sp", t, V(src_ap.partition_broadcast(128), src_v.bufs))
        return t

    def rstd(self, ss, n):
        self.ts("dve", ss, ss, 1.0 / n, EPS, ALU.mult, ALU.add)
        self.act(ss, ss, AF.Sqrt)
        self.P.op("dve", lambda e: e.reciprocal(ss.ap, ss.ap), r=[ss], w=[ss])

    def setup_consts(self):
        P = self.P
        cf = P.sb("cf", [128, 4, 128], F32)
        P.dma("sp", cf, self.consts)
        self.ident_f = cf[:, 0, :]
        self.tri = cf[:, 1, :]
        self.ones_f = cf[:, 3, :]
        self.identb = P.sb("identb", [128, 128], BF16)
        self.cp("dve", self.identb, cf[:, 0, :])
        self.negm4 = P.sb("negm4", [128, 4, 128], BF16)
        self.cp("dve", self.negm4, V(cf.ap[:, 2, :].unsqueeze(1).to_broadcast([128, 4, 128]), cf.bufs))
        self.ones_b = P.sb("ones_b", [128, 1], BF16)
        self.cp("dve", self.ones_b, cf[:, 3, 0:1])

    def phase0(self, l, mod):
        with self.P.scope():
            self.phase0_body(l, mod, 0)

    def mpart(self, mod, region):
        mp = self.__dict__.setdefault("_mp", {})
        key = (id(mod.bufs[0]), region)
        if key not in mp:
            mp[key] = (Buf("modp%d" % region), mod)
        return V(mod.ap, [mp[key][0]])

    def mall(self, mod):
        return V(mod.ap, [self.mpart(mod, r).bufs[0] for r in range(3)])

    def phase0_body(self, l, mod, pbase, nslab=2):
        P = self.P
        if True:
            cs = P.sb("cs", [128, 16], F32)
            P.dma("sp", cs, self.cT)
            self.act(cs, cs, AF.Silu)
            condB = P.sb("condB", [128, 16, 128], BF16)
            self.cp("dve", condB, V(cs.ap.unsqueeze(2).to_broadcast([128, 16, 128]), cs.bufs))
            adab = self.rowload("adab", self.ada_b.ap[l:l + 1, :], 5120, self.ada_b)
            nw = self.rowload("nw", self.norm_w.ap[l:l + 1, :], D, self.norm_w)
            wv = self.ada_w.ap[l].rearrange("(p k) n -> p k n", k=16)
            slabs = [P.sb("adaw%d" % i, [128, 16, 512], BF16) for i in range(nslab)]
            for it, n in enumerate([4, 5, 6, 7, 0, 1, 2, 3, 8, 9]):
                sl = slabs[it % nslab]
                P.dma("pool", sl, V(wv[:, :, n * 512:(n + 1) * 512], self.ada_w.bufs))
                ps = self.pb[pbase + it % 2]
                for k in range(16):
                    self.mm(ps, condB[:, k, :], sl[:, k, :], start=(k == 0), stop=(k == 15))
                mr = self.mpart(mod, n // 4)
                self.tt("dve", mr[:, n * 512:(n + 1) * 512], ps, adab[:, n * 512:(n + 1) * 512], ALU.add)
                if n == 7:
                    self.stt("dve", mr[:, D:2 * D], mr[:, D:2 * D], 1.0, nw, ALU.add, ALU.mult)

    def phaseA(self, srcf, mod, phase0_l=None):
        P = self.P
        with P.scope():
            if phase0_l is not None:
                self.phase0_body(phase0_l, mod, 0, nslab=3)
            xt = [P.sb("xt%d" % i, [128, D], F32) for i in range(3)]
            hn_ = [P.sb("hn%d" % i, [128, D], F32) for i in range(2)]
            hb_ = [P.sb("hb%d" % i, [128, D], BF16) for i in range(2)]
            junk_ = [P.sb("junkA%d" % i, [128, D], BF16) for i in range(2)]
            ss = [P.sb("ssA%d" % i, [128, 1], F32) for i in range(2)]
            stage = [P.sb("hTst%d" % i, [128, 16, 512], BF16) for i in range(2)]
            for i in range(self.NST):
                sc, c = i // 4, i % 4
                x_ = xt[i % 3]
                hn, hb, junk = hn_[i % 2], hb_[i % 2], junk_[i % 2]
                P.dma("sp", x_.re("p (r n) -> p r n", r=2), srcf(i))
                s_ = ss[i % 2]
                self.act(junk, x_, AF.Square, accum=s_)
                self.rstd(s_, D)
                self.stt("dve", hn, x_, s_, self.mpart(mod, 1)[:, D:2 * D], ALU.mult, ALU.mult)
                self.tt("pool", hb, hn, self.mpart(mod, 0)[:, 0:D], ALU.add)
                st = stage[sc % 2]
                for half in range(2):
                    for kk in range(8):
                        k = half * 8 + kk
                        self.tr(self.pT[:, kk * 128:(kk + 1) * 128], hb[:, k * 128:(k + 1) * 128])
                    self.cp("act" if half == 0 else "dve", st[:, half * 8:(half + 1) * 8, c * 128:(c + 1) * 128],
                            self.pT.re("p (k t) -> p k t", k=8))
                if c == 3:
                    self.store(V(self.hT_d[sc], [self.hT_bufs[sc]]), st)

    def conv_fm(self, hs, wlist, U, XB, cw, cb, cidx, first):
        P = self.P
        nj = len(wlist)
        for j in range(nj):
            ps = self.pb[j % 2]
            for k in range(16):
                self.mm(ps, wlist[j][:, k, :], hs[:, k, :], start=(k == 0), stop=(k == 15))
            Uj = U[j % len(U)]
            Hj = self.Uh[j]
            if first:
                self.P.op("pool", lambda e, a=Uj.ap[:, 0:3]: e.memset(a, 0.0), w=[Uj])
            else:
                self.cp("pool", Uj[:, 0:3], Hj)
            self.cp("act", Uj[:, 3:515], ps)
            acc = self.cacc[j % 2]
            ci = cidx[j]
            self.ts("dve", acc, Uj[:, 0:512], cw[:, ci, 0:1], cb[:, ci:ci + 1], ALU.mult, ALU.add)
            for k in range(1, 4):
                self.stt("dve", acc, Uj[:, k:k + 512], cw[:, ci, k:k + 1], acc, ALU.mult, ALU.add)
            self.act(XB[:, j, :], acc, AF.Silu)
            self.cp("pool", Hj, Uj[:, 512:515])

    def store_yT(self, GN, i, cbase):
        P = self.P
        for q in range(4):
            self.tr(self.pT2[:, q * 128:(q + 1) * 128], GN[:, q * 128:(q + 1) * 128])
        ys = self.yst[i % 2]
        self.cp("act", ys, self.pT2[:, 0:512].re("p (q t) -> p q t", q=4))
        gl = cbase
        pt, off = i // self.NY, (i % self.NY) * 128
        self.store(V(self.yT_loc[gl, pt, off:off + 128, :], [self.yT_loc_bufs[gl][pt]]), ys.re("p q t -> p (q t)"))
        if i % self.NY == self.NY - 1:
            self.allgather(self.yT_loc[gl, pt], self.yT_loc_bufs[gl][pt], self.yT_all[gl, pt], self.yT_all_bufs[gl][pt],
                           2 * self.RY * 512 * 2)

    def phaseB_ssd(self):
        try:
            self.phaseB_ssd_()
        except Cut:
            self.P.stacks.pop()
            self.P.barrier()

    def phaseB_ssd_(self):
        P = self.P
        T = self.T
        with P.scope():
            wv = self.s_win.ap.rearrange("(k p) n -> p k n", p=128)
            Wz = [P.sb("Wz%d" % i, [128, 16, 512], BF16) for i in range(2)]
            Wx = [P.sb("Wx%d" % i, [128, 16, 512], BF16) for i in range(2)]
            Wbc = [P.sb("Wbc%d" % i, [128, 16, 256], BF16) for i in range(2)]
            Wdt = P.sb("WdtA", [128, 16, 32], BF16)
            hTs = [P.sb("hTs%d" % i, [128, 16, 512], BF16) for i in range(2)]
            cw = P.sb("cw", [128, 24, 4], F32)
            cb = P.sb("cb", [128, 24], F32)
            P.dma("sp", cw, self.s_cw)
            P.dma("sp", cb, self.s_cb)
            dtb = self.rowload("dtb", self.s_dtb.ap, 32, self.s_dtb)
            aneg = self.rowload("aneg", self.s_alog.ap, 32, self.s_alog)
            self.act(aneg, aneg, AF.Exp)
            self.ts("dve", aneg, aneg, -1.0, None, ALU.mult)
            dsk = self.rowload("dsk", self.s_d.ap, 32, self.s_d)
            nwr = P.sb("nwr", [128, 512], F32)
            U = [P.sb("U%d" % j, [128, 515], F32) for j in range(3)]
            self.Uh = [P.sb("Uh%d" % j, [128, 3], F32) for j in range(6)]
            self.cacc = [P.sb("cacc%d" % j, [128, 512], F32) for j in range(2)]
            XB_ = [P.sb("XB%d" % q_, [128, 6, 512], BF16) for q_ in range(2)]
            SZ_ = [P.sb("SZ%d" % q_, [128, 4, 512], BF16) for q_ in range(2)]
            DT_ = [P.sb("DT%d" % q_, [128, 4, 32], F32) for q_ in range(self.NSC)]
            Aa_ = [P.sb("Aa%d" % q_, [128, 4, 8], F32) for q_ in range(2)]
            XT_ = [P.sb("XT%d" % q_, [128, 512], BF16) for q_ in range(2)]
            BT_ = [P.sb("BT%d" % q_, [128, 128], BF16) for q_ in range(2)]
            cbm_ = [P.sb("cbm%d" % q_, [128, 128], F32) for q_ in range(2)]
            arhs_ = [P.sb("arhs", [128, 8, 128], F32)] * 2
            acs_ = [P.sb("acs%d" % q_, [128, 8], F32) for q_ in range(2)]
            eacs_ = [P.sb("eacs%d" % q_, [128, 8], F32) for q_ in range(2)]
            etot_ = [P.sb("etot%d" % q_, [128, 8], F32) for q_ in range(2)]
            wend_ = [P.sb("wend%d" % q_, [128, 8], F32) for q_ in range(2)]
            segs_ = [P.sb("segs", [128, 4, 128], F32)] * 2
            MT_ = [P.sb("MT%d" % q_, [128, 8, 128], BF16) for q_ in range(2)]
            XDT_ = [P.sb("XDT%d" % q_, [128, 512], BF16) for q_ in range(2)]
            XW_ = [P.sb("XW%d" % q_, [128, 512], BF16) for q_ in range(2)]
            S = P.sb("S", [128, 512], F32)
            Sb = P.sb("Sb", [128, 512], BF16)
            t1_ = [P.sb("t1%d" % q_, [128, 512], F32) for q_ in range(2)]
            t2_ = [P.sb("t2", [128, 512], F32)] * 2
            ssq_ = [P.sb("ssq%d" % q_, [128, 1], F32) for q_ in range(2)]
            GN_ = [P.sb("GN%d" % q_, [128, 512], BF16) for q_ in range(2)]
            self.yst = [P.sb("yst%d" % i, [128, 4, 128], BF16) for i in range(2)]
            pmisc, pseg, pyd, pyo, pds, pz = self.pb[3], self.pb[4], self.pb[5], self.pb[6], self.pb[2], self.pb[2]

            def loadW(g):
                b = g % 2
                o = g * 1288
                P.dma("pool", Wz[b], V(wv[:, :, o:o + 512], self.s_win.bufs))
                P.dma("pool", Wx[b], V(wv[:, :, o + 512:o + 1024], self.s_win.bufs))
                P.dma("pool", Wbc[b], V(wv[:, :, o + 1024:o + 1280], self.s_win.bufs))

            for g_ in range(4):
                P.dma("pool", Wdt[:, :, g_ * 8:(g_ + 1) * 8], V(wv[:, :, g_ * 1288 + 1280:g_ * 1288 + 1288], self.s_win.bufs))
            loadW(0)
            for g in range(4 if self.cut is None else 1):
                b = g % 2
                if g + 1 < 4:
                    loadW(g + 1)
                P.op("pool", lambda e: e.memset(S.ap, 0.0), w=[S])
                P.op("pool", lambda e: e.memset(Sb.ap, 0.0), w=[Sb])
                P.dma("sp", nwr, V(self.s_nw.ap[:, g * 512:(g + 1) * 512].partition_broadcast(128), self.s_nw.bufs))
                wl = [Wx[b][:, :, j * 128:(j + 1) * 128] for j in range(4)] + [Wbc[b][:, :, 0:128], Wbc[b][:, :, 128:256]]
                cidx = [g * 4 + j for j in range(4)] + [16 + g, 20 + g]
                for sc in range(self.NSC):
                    hs = hTs[sc % 2]
                    XB = XB_[sc % 2]
                    SZ = SZ_[sc % 2]
                    DT = DT_[sc]
                    Aa = Aa_[sc % 2]
                    gs = slice(g * 8, (g + 1) * 8)
                    P.dma("sp", hs, V(self.hT_d[sc], [self.hT_bufs[sc]]))
                    self.ck(1)
                    self.conv_fm(hs, wl, U, XB, cw, cb, cidx, sc == 0)
                    self.ck(2)
                    for c in range(4):
                        tok = slice(c * 128, (c + 1) * 128)
                        for k in range(16):
                            self.mm(pz, hs[:, k, tok], Wz[b][:, k, :], start=(k == 0), stop=(k == 15))
                        self.act(SZ[:, c, :], pz, AF.Silu)
                        if g == 0:
                            for k in range(16):
                                self.mm(pmisc[:, 256:288], hs[:, k, tok], Wdt[:, k, :], start=(k == 0), stop=(k == 15))
                            self.tt("dve", DT[:, c, :], pmisc[:, 256:288], dtb, ALU.add)
                            self.act(DT[:, c, :], DT[:, c, :], AF.Exp)
                            self.act(DT[:, c, :], DT[:, c, :], AF.Ln, bias=1.0)
                        self.tt("dve", Aa[:, c, :], DT[:, c, gs], aneg[:, gs], ALU.mult)
                    self.ck(3)
                    for c in range(4):
                        i = sc * 4 + c
                        tok = slice(c * 128, (c + 1) * 128)
                        a = Aa[:, c, :]
                        XT = XT_[i % 2]
                        BT = BT_[i % 2]
                        cbm = cbm_[i % 2]
                        arhs = arhs_[i % 2]
                        acs = acs_[i % 2]
                        eacs = eacs_[i % 2]
                        etot = etot_[i % 2]
                        wend = wend_[i % 2]
                        segs = segs_[i % 2]
                        MT = MT_[i % 2]
                        XDT = XDT_[i % 2]
                        XW = XW_[i % 2]
                        t1 = t1_[i % 2]
                        t2 = t2_[i % 2]
                        ssq = ssq_[i % 2]
                        GN = GN_[i % 2]
                        junk = t2
                        for j in range(5):
                            self.tr(self.pT[:, j * 128:(j + 1) * 128], XB[:, j, tok])
                        self.cp("act", XT, self.pT[:, 0:512])
                        self.cp("dve", BT, self.pT[:, 512:640])
                        self.ck(4)
                        self.mm(pmisc[:, 128:256], XB[:, 4, tok], XB[:, 5, tok])
                        self.tt("dve", cbm, pmisc[:, 128:256], self.tri, ALU.mult)
                        self.tt("dve", arhs, V(self.tri.ap.unsqueeze(1).to_broadcast([128, 8, 128]), self.tri.bufs),
                                V(a.ap.unsqueeze(2).to_broadcast([128, 8, 128]), a.bufs), ALU.mult)
                        self.mm(pmisc[:, 8:16], self.tri, a)
                        self.mm(pmisc[:, 16:24], self.ones_f, a)
                        self.cp("act", acs, pmisc[:, 8:16])
                        self.act(eacs, pmisc[:, 8:16], AF.Exp)
                        self.act(etot, pmisc[:, 16:24], AF.Exp)
                        self.tt("dve", wend, pmisc[:, 16:24], acs, ALU.subtract)
                        self.act(wend, wend, AF.Exp)
                        self.ck(5)
                        for hh in range(2):
                            self.mm(pseg, self.ones_f, arhs[:, hh * 4:(hh + 1) * 4, :].re("p a b -> p (a b)"), start=True, stop=False)
                            self.mm(pseg, self.identb, self.negm4.re("p a b -> p (a b)"), start=False, stop=True)
                            self.tt("dve", segs, pseg.re("p (a b) -> p a b", a=4),
                                    V(acs.ap[:, hh * 4:(hh + 1) * 4].unsqueeze(2).to_broadcast([128, 4, 128]), acs.bufs), ALU.subtract)
                            self.act(segs, segs, AF.Exp)
                            self.tt("pool", MT[:, hh * 4:(hh + 1) * 4, :], segs,
                                    V(cbm.ap.unsqueeze(1).to_broadcast([128, 4, 128]), cbm.bufs), ALU.mult)
                        self.ck(6)
                        dtc = DT[:, c, gs]
                        XT3 = XT.re("p (r q) -> p r q", r=8)
                        self.tt("dve", XDT.re("p (r q) -> p r q", r=8), XT3,
                                V(dtc.ap.unsqueeze(2).to_broadcast([128, 8, 64]), dtc.bufs), ALU.mult)
                        self.tt("pool", XW.re("p (r q) -> p r q", r=8), XDT.re("p (r q) -> p r q", r=8),
                                V(wend.ap.unsqueeze(2).to_broadcast([128, 8, 64]), wend.bufs), ALU.mult)
                        for r in range(8):
                            self.mm(pyd[:, r * 64:(r + 1) * 64], MT[:, r, :], XDT[:, r * 64:(r + 1) * 64])
                        self.mm(pyo, XB[:, 5, tok], Sb)
                        self.mm(pds, BT, XW)
                        self.ck(7)
                        self.tt("dve", t1.re("p (r q) -> p r q", r=8), pyo.re("p (r q) -> p r q", r=8),
                                V(eacs.ap.unsqueeze(2).to_broadcast([128, 8, 64]), eacs.bufs), ALU.mult)
                        self.tt("dve", t1, pyd, t1, ALU.add)
                        self.tt("pool", t2.re("p (r q) -> p r q", r=8), XT3,
                                V(dsk.ap[:, g * 8:(g + 1) * 8].unsqueeze(2).to_broadcast([128, 8, 64]), dsk.bufs), ALU.mult)
                        self.tt("pool", t1, t1, t2, ALU.add)
                        self.tt("dve", S.re("p (r q) -> p r q", r=8), S.re("p (r q) -> p r q", r=8),
                                V(etot.ap.unsqueeze(2).to_broadcast([128, 8, 64]), etot.bufs), ALU.mult)
                        self.tt("dve", S, pds, S, ALU.add)
                        self.cp("act", Sb, S)
                        self.ck(8)
                        self.tt("dve", t1, t1, SZ[:, c, :], ALU.mult)
                        self.act(junk, t1, AF.Square, accum=ssq)
                        self.rstd(ssq, 512)
                        self.stt("dve", GN, t1, ssq, nwr, ALU.mult, ALU.mult)
                        self.ck(9)
                        self.store_yT(GN, i, g)
                        self.ck(10)

    def phaseB_ml(self):
        P = self.P
        with P.scope():
            wv = self.m_win.ap.rearrange("(k p) n -> p k n", p=128)
            Wqk = P.sb("Wqk", [128, 16, 512], BF16)
            Wv = P.sb("Wv", [128, 16, 512], BF16)
            Wo = P.sb("Wo", [128, 16, 512], BF16)
            Wzz = P.sb("Wzz", [128, 16, 512], BF16)
            Wg = P.sb("WgA", [128, 16, 8], BF16)
            hTs = [P.sb("hTs%d" % i, [128, 16, 512], BF16) for i in range(2)]
            cw = P.sb("cw", [128, 16, 4], F32)
            cb = P.sb("cb", [128, 16], F32)
            P.dma("sp", cw, self.m_cw)
            P.dma("sp", cb, self.m_cb)
            ibr = self.rowload("ibr", self.m_ib.ap, 4, self.m_ib)
            self.ts("dve", ibr, ibr, float(np.log(1.0 / 16.0)), None, ALU.add)
            nfb = self.rowload("nfb", self.m_fb.ap, 4, self.m_fb)
            self.ts("dve", nfb, nfb, -1.0, None, ALU.mult)
            nwr = P.sb("nwr", [128, 512], F32)
            U = [P.sb("U%d" % j, [128, 515], F32) for j in range(3)]
            self.Uh = [P.sb("Uh%d" % j, [128, 3], F32) for j in range(4)]
            self.cacc = [P.sb("cacc%d" % j, [128, 512], F32) for j in range(2)]
            QK_ = [P.sb("QK%d" % q_, [128, 4, 512], BF16) for q_ in range(2)]
            Vv_ = [P.sb("Vv%d" % q_, [128, 4, 512], BF16) for q_ in range(2)]
            SO_ = [P.sb("SO", [128, 4, 512], F32)] * 2
            SZ_ = [P.sb("SZ", [128, 4, 512], F32)] * 2
            IG_ = [P.sb("IG%d" % q_, [128, 4, 4], F32) for q_ in range(self.NSC)]
            Aa_ = [P.sb("Aa%d" % q_, [128, 4, 4], F32) for q_ in range(self.NSC)]
            KT_ = [P.sb("KT%d" % q_, [128, 256], BF16) for q_ in range(2)]
            sm_ = [P.sb("sm%d" % q_, [128, 128], F32) for q_ in range(2)]
            arhs_ = [P.sb("arhs%d" % q_, [128, 128], F32) for q_ in range(2)]
            bcum_ = [P.sb("bcum%d" % q_, [128, 1], F32) for q_ in range(2)]
            ebc_ = [P.sb("ebc%d" % q_, [128, 1], F32) for q_ in range(2)]
            etot_ = [P.sb("etot%d" % q_, [128, 1], F32) for q_ in range(2)]
            w2_ = [P.sb("w2%d" % q_, [128, 1], F32) for q_ in range(2)]
            w2b_ = [P.sb("w2b%d" % q_, [128, 1], BF16) for q_ in range(2)]
            segs_ = [P.sb("segs%d" % q_, [128, 128], F32) for q_ in range(2)]
            PT_ = [P.sb("PT%d" % q_, [128, 128], BF16) for q_ in range(2)]
            VW_ = [P.sb("VW%d" % q_, [128, 512], BF16) for q_ in range(2)]
            C = P.sb("C", [128, 2, 512], F32)
            Cb = P.sb("Cb", [128, 2, 512], BF16)
            nst = P.sb("nst", [128, 2], F32)
            nb = P.sb("nb", [128, 2], BF16)
            den_ = [P.sb("den%d" % q_, [128, 1], F32) for q_ in range(2)]
            rinv_ = [P.sb("rinv%d" % q_, [128, 1], F32) for q_ in range(2)]
            c2_ = [P.sb("c2%d" % q_, [128, 1], F32) for q_ in range(2)]
            t1_ = [P.sb("t1%d" % q_, [128, 512], F32) for q_ in range(2)]
            t2_ = [P.sb("t2%d" % q_, [128, 512], F32) for q_ in range(2)]
            junk_ = [P.sb("junkB%d" % q_, [128, 512], F32) for q_ in range(2)]
            ssq_ = [P.sb("ssq%d" % q_, [128, 1], F32) for q_ in range(2)]
            GN_ = [P.sb("GN%d" % q_, [128, 512], BF16) for q_ in range(2)]
            self.yst = [P.sb("yst%d" % i, [128, 4, 128], BF16) for i in range(2)]
            pmisc, pyd, pyo = self.pb[3], self.pb[5], self.pb[6]
            ptms = [self.pb[2], self.pb[4]]
            pseg = self.pb[3][:, 256:384]
            pdc = [self.pb[0], self.pb[1]]
            pg3 = pmisc[:, 40:48].re("p (h t) -> p h t", t=2)
            for h_ in range(4):
                P.dma("pool", Wg[:, :, 2 * h_:2 * h_ + 2], V(wv[:, :, h_ * 2050 + 2048:h_ * 2050 + 2050], self.m_win.bufs))
            for h in range(4):
                o = h * 2050
                P.dma("pool", Wqk, V(wv[:, :, o:o + 512], self.m_win.bufs))
                P.dma("pool", Wv, V(wv[:, :, o + 512:o + 1024], self.m_win.bufs))
                P.dma("pool", Wo, V(wv[:, :, o + 1024:o + 1536], self.m_win.bufs))
                P.dma("pool", Wzz, V(wv[:, :, o + 1536:o + 2048], self.m_win.bufs))
                P.dma("sp", nwr, V(self.m_nw.ap[:, h * 512:(h + 1) * 512].partition_broadcast(128), self.m_nw.bufs))
                P.op("pool", lambda e: e.memset(C.ap, 0.0), w=[C])
                P.op("pool", lambda e: e.memset(Cb.ap, 0.0), w=[Cb])
                P.op("pool", lambda e: e.memset(nst.ap, 0.0), w=[nst])
                P.op("pool", lambda e: e.memset(nb.ap, 0.0), w=[nb])
                wl = [Wqk[:, :, j * 128:(j + 1) * 128] for j in range(4)]
                cidx = [2 * h, 2 * h + 1, 8 + 2 * h, 8 + 2 * h + 1]
                for sc in range(self.NSC):
                    hs = hTs[sc % 2]
                    QK = QK_[sc % 2]
                    Vv = Vv_[sc % 2]
                    SO = SO_[sc % 2]
                    SZ = SZ_[sc % 2]
                    IG = IG_[sc]
                    Aa = Aa_[sc]
                    P.dma("sp", hs, V(self.hT_d[sc], [self.hT_bufs[sc]]))
                    self.conv_fm(hs, wl, U, QK, cw, cb, cidx, sc == 0)
                    for c in range(4):
                        tok = slice(c * 128, (c + 1) * 128)
                        for wi_, (W_, dst, fn) in enumerate(((Wv, Vv, None), (Wo, SO, AF.Sigmoid), (Wzz, SZ, AF.Silu))):
                            ptm = ptms[(c * 3 + wi_) % 2]
                            for k in range(16):
                                self.mm(ptm, hs[:, k, tok], W_[:, k, :], start=(k == 0), stop=(k == 15))
                            if fn is None:
                                self.cp("act", dst[:, c, :], ptm)
                            else:
                                self.act(dst[:, c, :], ptm, fn)
                        if h == 0:
                            for k in range(16):
                                self.mm(pmisc[:, 40:48], hs[:, k, tok], Wg[:, k, :], start=(k == 0), stop=(k == 15))
                            self.tt("dve", IG[:, c, :], pg3[:, :, 0], ibr, ALU.add)
                            self.stt("dve", Aa[:, c, :], pg3[:, :, 1], -1.0, nfb, ALU.mult, ALU.add)
                            self.act(Aa[:, c, :], Aa[:, c, :], AF.Exp)
                            self.act(Aa[:, c, :], Aa[:, c, :], AF.Ln, bias=1.0)
                            self.ts("dve", Aa[:, c, :], Aa[:, c, :], -1.0, None, ALU.mult)
                    for c in range(4):
                        i = sc * 4 + c
                        tok = slice(c * 128, (c + 1) * 128)
                        a = Aa[:, c, h:h + 1]
                        KT = KT_[i % 2]
                        sm = sm_[i % 2]
                        arhs = arhs_[i % 2]
                        bcum = bcum_[i % 2]
                        ebc = ebc_[i % 2]
                        etot = etot_[i % 2]
                        w2 = w2_[i % 2]
                        w2b = w2b_[i % 2]
                        segs = segs_[i % 2]
                        PT = PT_[i % 2]
                        VW = VW_[i % 2]
                        den = den_[i % 2]
                        rinv = rinv_[i % 2]
                        c2 = c2_[i % 2]
                        t1 = t1_[i % 2]
                        t2 = t2_[i % 2]
                        junk = junk_[i % 2]
                        ssq = ssq_[i % 2]
                        GN = GN_[i % 2]
                        ig = IG[:, c, h:h + 1]
                        for j in range(2):
                            self.tr(self.pT[:, j * 128:(j + 1) * 128], QK[:, 2 + j, tok])
                        self.cp("act", KT, self.pT[:, 0:256])
                        for j in range(2):
                            self.mm(pmisc[:, 128:256], QK[:, 2 + j, tok], QK[:, j, tok], start=(j == 0), stop=(j == 1))
                        self.tt("dve", sm, pmisc[:, 128:256], self.tri, ALU.mult)
                        self.ts("dve", arhs, self.tri, a, None, ALU.mult)
                        self.mm(pmisc[:, 8:9], self.tri, a)
                        self.mm(pmisc[:, 16:17], self.ones_f, a)
                        self.cp("act", bcum, pmisc[:, 8:9])
                        self.act(ebc, pmisc[:, 8:9], AF.Exp)
                        self.act(etot, pmisc[:, 16:17], AF.Exp)
                        self.tt("dve", w2, pmisc[:, 16:17], bcum, ALU.subtract)
                        self.act(w2, w2, AF.Exp, bias=ig)
                        self.cp("dve", w2b, w2)
                        self.mm(pseg, self.ones_f, arhs, start=True, stop=False)
                        self.mm(pseg, self.identb, self.negm4[:, 0, :], start=False, stop=True)
                        self.ts("dve", segs, pseg, bcum, None, ALU.subtract)
                        self.act(segs, segs, AF.Exp, bias=ig)
                        self.tt("pool", PT, segs, sm, ALU.mult)
                        self.ts("dve", VW, Vv[:, c, :], w2, None, ALU.mult)
                        self.mm(pyd, PT, Vv[:, c, :])
                        for j in range(2):
                            self.mm(pyo, QK[:, j, tok], Cb[:, j, :], start=(j == 0), stop=(j == 1))
                        self.mm(pmisc[:, 24:25], PT, self.ones_b)
                        for j in range(2):
                            self.mm(pmisc[:, 25:26], QK[:, j, tok], nb[:, j:j + 1], start=(j == 0), stop=(j == 1))
                        for j in range(2):
                            self.mm(pdc[j], KT[:, j * 128:(j + 1) * 128], VW)
                            self.mm(pmisc[:, 32 + j:33 + j], KT[:, j * 128:(j + 1) * 128], w2b)
                        self.tt("dve", den, pmisc[:, 25:26], ebc, ALU.mult)
                        self.tt("dve", den, pmisc[:, 24:25], den, ALU.add)
                        self.ts("dve", c2, den, -1.0, None, ALU.mult)
                        self.tt("dve", den, den, c2, ALU.max)
                        self.ts("dve", den, den, 1.0, None, ALU.max)
                        self.P.op("dve", lambda e, o_=rinv.ap, i_=den.ap: e.reciprocal(o_, i_), r=[den], w=[rinv])
                        self.tt("dve", c2, ebc, rinv, ALU.mult)
                        self.ts("dve", t2, pyo, c2, None, ALU.mult)
                        self.stt("dve", t1, pyd, rinv, t2, ALU.mult, ALU.add)
                        for j in range(2):
                            self.stt("dve", C[:, j, :], C[:, j, :], etot, pdc[j], ALU.mult, ALU.add)
                            self.cp("act", Cb[:, j, :], C[:, j, :])
                        self.stt("dve", nst, nst, etot, pmisc[:, 32:34], ALU.mult, ALU.add)
                        self.cp("dve", nb, nst)
                        self.tt("pool", t1, t1, SO[:, c, :], ALU.mult)
                        self.act(junk, t1, AF.Square, accum=ssq)
                        self.rstd(ssq, 512)
                        self.stt("dve", t1, t1, ssq, nwr, ALU.mult, ALU.mult)
                        self.tt("pool", GN, t1, SZ[:, c, :], ALU.mult)
                        self.store_yT(GN, i, h)

    def phaseC(self, wout, resf, l, mod, next_mod=False, tail=None):
        P = self.P
        R = self.NST * 128
        with P.scope():
            if next_mod:
                mod2 = P.sb("mod2", [128, 5120], F32)
                self.phase0_body(l, mod2, 4)
            wvv = wout.ap.rearrange("(c p) n -> p c n", p=128)
            Whs = [P.sb("Wh%d" % q, [128, 8, 1024], BF16) for q in range(4)]
            ys = [P.sb("ys%d" % i, [128, 32, 128], BF16) for i in range(2 if next_mod else 3)]
            xt = [P.sb("xtC%d" % i, [128, 1024], F32) for i in range(2)]
            xo = [P.sb("xoC%d" % i, [128, 1024], F32) for i in range(2)]
            for q in range(4):
                P.dma("pool", Whs[q], V(wvv[:, q * 8:(q + 1) * 8, :], wout.bufs))
            for i in range(self.NST):
                y_ = ys[i % len(ys)]
                pt, off = i // self.NY, (i % self.NY) * 128
                for rk in range(2):
                    src = self.yT_all[:, pt, rk * self.RY + off: rk * self.RY + off + 128, :].rearrange("g p n -> p g n")
                    P.dma("sp", y_[:, rk * 16:(rk + 1) * 16, :].re("p (g q) t -> p g (q t)", g=4),
                          V(src, [self.yT_all_bufs[g_][pt] for g_ in range(4)]), nbytes=128 * 4 * 512 * 2)
                x_ = xt[i % 2]
                P.dma("sp", x_, resf(i))
                o_ = xo[i % 2]
                for n2 in range(2):
                    ps = self.pb[(i * 2 + n2) % 4]
                    for c in range(32):
                        self.mm(ps, y_[:, c, :], Whs[c // 8][:, c % 8, n2 * 512:(n2 + 1) * 512], start=(c == 0), stop=(c == 31))
                    col = slice(2 * D + n2 * 512, 2 * D + (n2 + 1) * 512)
                    self.tt("dve", o_[:, n2 * 512:(n2 + 1) * 512], ps, self.mpart(mod, 2)[:, col], ALU.mult)
                    self.tt("pool", o_[:, n2 * 512:(n2 + 1) * 512], o_[:, n2 * 512:(n2 + 1) * 512],
                            x_[:, n2 * 512:(n2 + 1) * 512], ALU.add)
                q, off = i // self.NQ, (i % self.NQ) * 128
                self.store(V(self.xl[l][q, off:off + 128, :], [self.xl_bufs[l][q]]), o_)
                if i % self.NQ == self.NQ - 1:
                    self.allgather(self.xl[l][q], self.xl_bufs[l][q], self.xa[l][q], self.xa_bufs[l][q],
                                   2 * self.NQ * 128 * 1024 * 4)
            if next_mod:
                self.cp("dve", self.mall(mod), self.mall(mod2))
            if tail is not None:
                tail()

    def res_xh(self, i):
        return V(self.xh.ap[i * 128:(i + 1) * 128, :], self.xh.bufs)

    def res_loc(self, l):
        def f(i):
            q, off = i // self.NQ, (i % self.NQ) * 128
            return V(self.xl[l][q, off:off + 128, :], [self.xl_bufs[l][q]])
        return f

    def phaseD(self, srcf):
        with self.P.scope():
            self.phaseD_body(srcf)

    def phaseD_body(self, srcf):
        P = self.P
        if True:
            fw_ = self.rowload("fnw", self.fnorm_w.ap, D, self.fnorm_w)
            xt = [P.sb("xtD%d" % i, [128, D], F32) for i in range(2)]
            ot = [P.sb("otD%d" % i, [128, D], F32) for i in range(2)]
            junk = P.sb("junkD", [128, D], BF16)
            ss = [P.sb("ssD%d" % i, [128, 1], F32) for i in range(2)]
            for i in range(self.NST):
                x_ = xt[i % 2]
                P.dma("sp", x_.re("p (r n) -> p r n", r=2), srcf(i))
                s_ = ss[i % 2]
                self.act(junk, x_, AF.Square, accum=s_)
                self.rstd(s_, D)
                self.stt("dve", ot[i % 2], x_, s_, fw_, ALU.mult, ALU.mult)
                self.store(V(self.out[i * 128:(i + 1) * 128, :], [self.out_bufs[i]]), ot[i % 2])

    def build(self, upto="all"):
        P = self.P
        mod = P.sb("mod", [128, 5120], F32)
        self.phaseA(self.src_x, mod, phase0_l=0)
        self.phaseB_ssd()
        self.phaseC(self.s_wout, self.res_xh, 1, mod, next_mod=(upto != "l1"))
        if upto == "l1":
            self.phaseD(self.src_all(1))
        else:
            self.phaseA(self.src_all(1), mod)
            self.phaseB_ml()
            self.phaseC(self.m_wout, self.res_loc(1), 2, mod, tail=lambda: self.phaseD_body(self.src_all(2)))
        P.op("sp", lambda e: e.nop(), r=[V(self.out, self.out_bufs)])
        P.emit(sched=self.sched)
        print("sim_time_ms", getattr(P, "sim_time", 0) / 1e6, {e: round(b / 1e6, 2) for e, b in getattr(P, "sim_busy", {}).items()})
        return self.nc


def _shared_inputs(inp, r):
    f = np.float32
    ii = np.eye(128, dtype=f)
    tri = np.triu(np.ones((128, 128), f))
    negm = np.where(np.arange(128)[None, :] < np.arange(128)[:, None], f(NEG), f(0)).astype(f)
    consts = np.stack([ii, tri, negm, np.ones((128, 128), f)], axis=1)
    ar = np.arange
    scols, sch = [], []
    for gl in range(4):
        g = 4 * r + gl
        scols += [g * 512 + ar(512), 4096 + g * 512 + ar(512), 8192 + g * 128 + ar(128), 9216 + g * 128 + ar(128),
                  10240 + g * 8 + ar(8)]
    scols = np.concatenate(scols)
    sch = [4 * (4 * r + gl) + j for gl in range(4) for j in range(4)] + [32 + 4 * r + gl for gl in range(4)] + \
          [40 + 4 * r + gl for gl in range(4)]
    mcols = []
    for hl in range(4):
        h = 4 * r + hl
        mcols += [h * 256 + ar(256), 2048 + h * 256 + ar(256), 4096 + h * 512 + ar(512), 8192 + h * 512 + ar(512),
                  12288 + h * 512 + ar(512), np.array([16384 + h, 16392 + h])]
    mcols = np.concatenate(mcols)
    mch = [2 * (4 * r + hl) + j for hl in range(4) for j in range(2)] + [16 + 2 * (4 * r + hl) + j for hl in range(4) for j in range(2)]
    acols = np.concatenate([ar(4096), 4096 + r * 1024 + ar(1024)])
    cw = lambda w: np.ascontiguousarray(w.T.reshape(-1, 128, 4).transpose(1, 0, 2))
    cbv = lambda v: np.ascontiguousarray(v.reshape(-1, 128).T)
    osl = slice(r * 1024, (r + 1) * 1024)
    hs32 = slice(32 * r, 32 * r + 32)
    return {
        "norm_w": inp["norm_w"], "final_norm_w": inp["final_norm_w"].reshape(1, -1),
        "ada_w": np.ascontiguousarray(inp["ada_w"][:, :, acols]), "ada_b": np.ascontiguousarray(inp["ada_b"][:, acols]),
        "ssd_w_in": np.ascontiguousarray(inp["ssd_w_in"][0][:, scols]),
        "ssd_w_out": np.ascontiguousarray(inp["ssd_w_out"][0][:, osl]),
        "ssd_cw": np.ascontiguousarray(cw(inp["ssd_conv_w"][0])[:, sch, :]),
        "ssd_cb": np.ascontiguousarray(cbv(inp["ssd_conv_b"][0])[:, sch]),
        "ssd_dt_bias": np.ascontiguousarray(inp["ssd_dt_bias"][:, hs32]),
        "ssd_a_log": np.ascontiguousarray(inp["ssd_a_log"][:, hs32]),
        "ssd_d": np.ascontiguousarray(inp["ssd_d"][:, hs32]),
        "ssd_norm_w": np.ascontiguousarray(inp["ssd_norm_w"][:, r * 2048:(r + 1) * 2048]),
        "ml_w_in": np.ascontiguousarray(inp["ml_w_in"][0][:, mcols]),
        "ml_w_out": np.ascontiguousarray(inp["ml_w_out"][0][:, osl]),
        "ml_cw": np.ascontiguousarray(cw(inp["ml_conv_w"][0])[:, mch, :]),
        "ml_cb": np.ascontiguousarray(cbv(inp["ml_conv_b"][0])[:, mch]),
        "ml_igate_b": np.ascontiguousarray(inp["ml_igate_b"][:, 4 * r:4 * r + 4]),
        "ml_fgate_b": np.ascontiguousarray(inp["ml_fgate_b"][:, 4 * r:4 * r + 4]),
        "ml_norm_w": np.ascontiguousarray(inp["ml_norm_w"][:, r * 2048:(r + 1) * 2048]),
        "consts": consts,
    }


def host_inputs(inp, b, r, shared):
    d = dict(shared[r])
    d["x"] = np.ascontiguousarray(inp["x"][b])
    d["x_half"] = np.ascontiguousarray(inp["x"][b][:, r * 1024:(r + 1) * 1024])
    d["cT"] = np.ascontiguousarray(inp["c"][b].reshape(128, 16))
    return d


_NC_CACHE = {}


def run(inp, upto="all", ncores=8, cut=None):
    inp = {k: np.asarray(v, dtype=np.float32) for k, v in inp.items()}
    B, T, _ = inp["x"].shape
    key = (T, upto, cut)
    if key not in _NC_CACHE:
        kb = KB(T)
        kb.cut = cut
        _NC_CACHE[key] = kb.build(upto)
    nc = _NC_CACHE[key]
    shared = [_shared_inputs(inp, r) for r in range(2)]
    in_maps = [host_inputs(inp, (i // 2) % B, i % 2, shared) for i in range(ncores)]
    res = run_bass_kernel_spmd(nc, in_maps, core_ids=list(range(ncores)))
    return np.stack([res.results[2 * b]["out"] for b in range(B)], axis=0)


def kernel(**inputs):
    return run(inputs, "all", 8).astype(np.float32)
```
